# Optimizing a Trainium2 kernel written in Bass

```python
import math
import jax, jax.numpy as jnp
from jax import lax
import numpy as np

D_MODEL = 2048
BATCH = 2
SEQ = 4096
DEPTH = 4

BRANCH_WIDTH = 1024
N_BRANCH = 3
GLA_HEADS = 4
GLA_DK = 128
GLA_DV = 256
GLA_RANK = 16
GLA_TAU = 16.0
GLA_CHUNK = 64
SWA_Q_HEADS = 16
SWA_KV_HEADS = 2
SWA_HEAD_DIM = 64
SWA_WINDOW = 128
MOBA_HEADS = 8
MOBA_HEAD_DIM = 128
MOBA_BLOCK = 256
MOBA_TOPK = 3
MOBA_Q_CHUNK = 16
DEEPNORM_ALPHA = (2 * DEPTH) ** 0.25
DEEPNORM_BETA = (8 * DEPTH) ** -0.25
LN_EPS = 1e-5
RMS_EPS = 1e-6

IN_SPLITS = (
    GLA_HEADS * GLA_DK, GLA_HEADS * GLA_DK, GLA_HEADS * GLA_DV, GLA_RANK, BRANCH_WIDTH,
    SWA_Q_HEADS * SWA_HEAD_DIM, SWA_KV_HEADS * SWA_HEAD_DIM, SWA_KV_HEADS * SWA_HEAD_DIM, BRANCH_WIDTH,
    MOBA_HEADS * MOBA_HEAD_DIM, MOBA_HEADS * MOBA_HEAD_DIM, MOBA_HEADS * MOBA_HEAD_DIM, BRANCH_WIDTH,
    N_BRANCH * D_MODEL,
)
VALUE_SEGMENTS = (2, 7, 11)
D_IN = sum(IN_SPLITS)

kernel_name = "hybrid_gla_swa_moba_block"


def _layer_norm(h, g, b):
    h32 = h.astype(jnp.float32)
    mu = h32.mean(-1, keepdims=True)
    var = jnp.square(h32 - mu).mean(-1, keepdims=True)
    return ((h32 - mu) * lax.rsqrt(var + LN_EPS) * g.astype(jnp.float32) + b.astype(jnp.float32)).astype(h.dtype)


def _gla_branch(q, k, v, g, norm_g):
    B, S, H, dk = q.shape
    dv = v.shape[-1]
    nc = S // GLA_CHUNK

    def to_chunks(t):
        t = t.astype(jnp.float32).reshape(B, nc, GLA_CHUNK, H, t.shape[-1])
        return t.transpose(1, 0, 3, 2, 4)

    qc, kc, vc, gc = to_chunks(q * (dk ** -0.5)), to_chunks(k), to_chunks(v), to_chunks(g)
    causal = jnp.tril(jnp.ones((GLA_CHUNK, GLA_CHUNK), dtype=bool))

    def step(state, inp):
        qi, ki, vi, gi = inp
        b = jnp.cumsum(gi, axis=2)
        b_last = b[:, :, -1:, :]
        o_inter = jnp.einsum('bhcd,bhde->bhce', qi * jnp.exp(b), state)
        diff = b[:, :, :, None, :] - b[:, :, None, :, :]
        decay = jnp.exp(jnp.where(causal[None, None, :, :, None], diff, -jnp.inf))
        attn = jnp.einsum('bhid,bhjd,bhijd->bhij', qi, ki, decay)
        o = o_inter + jnp.einsum('bhij,bhje->bhie', attn, vi)
        new_state = (jnp.exp(b_last[:, :, 0, :])[..., None] * state
                     + jnp.einsum('bhcd,bhce->bhde', ki * jnp.exp(b_last - b), vi))
        return new_state, o

    state0 = jnp.zeros((B, H, dk, dv), jnp.float32)
    _, o = lax.scan(step, state0, (qc, kc, vc, gc))
    o = o.transpose(1, 0, 3, 2, 4).reshape(B, S, H, dv)
    o = o * lax.rsqrt(jnp.mean(jnp.square(o), axis=-1, keepdims=True) + RMS_EPS) * norm_g.astype(jnp.float32)
    return o.reshape(B, S, H * dv).astype(v.dtype)


def _swa_branch(q, k, v, sinks):
    B, S, Hq, hd = q.shape
    Hkv = k.shape[2]
    G = Hq // Hkv
    W = SWA_WINDOW
    nb = S // W
    qb = q.reshape(B, nb, W, Hkv, G, hd)
    kb = k.reshape(B, nb, W, Hkv, hd)
    vb = v.reshape(B, nb, W, Hkv, hd)
    shift = lambda t: jnp.concatenate([jnp.zeros_like(t[:, :1]), t[:, :-1]], axis=1)
    kk = jnp.concatenate([shift(kb), kb], axis=2)
    vv = jnp.concatenate([shift(vb), vb], axis=2)
    s = jnp.einsum('bnqhgd,bnkhd->bnhgqk', qb, kk).astype(jnp.float32) * (hd ** -0.5)
    qpos = jnp.arange(W)[:, None] + W
    kpos = jnp.arange(2 * W)[None, :]
    rel = qpos - kpos
    band = (rel >= 0) & (rel < W)
    mask = band[None] & ((jnp.arange(nb) > 0)[:, None, None] | (kpos >= W)[None])
    s = jnp.where(mask[None, :, None, None], s, -jnp.inf)
    sink = jnp.broadcast_to(sinks.astype(jnp.float32).reshape(1, 1, Hkv, G, 1, 1), s.shape[:-1] + (1,))
    p = jax.nn.softmax(jnp.concatenate([s, sink], axis=-1), axis=-1)[..., :-1]
    o = jnp.einsum('bnhgqk,bnkhd->bnqhgd', p.astype(v.dtype), vv)
    return o.reshape(B, S, Hq * hd)


def _moba_branch(q, k, v):
    B, S, H, hd = q.shape
    BLK = MOBA_BLOCK
    QC = MOBA_Q_CHUNK
    nblk = -(-S // BLK)
    Sp = nblk * BLK
    padw = ((0, 0), (0, Sp - S), (0, 0), (0, 0))
    q, k, v = [jnp.pad(t, padw).transpose(0, 2, 1, 3) for t in (q, k, v)]
    kb = k.reshape(B, H, nblk, BLK, hd)
    vb = v.reshape(B, H, nblk, BLK, hd)
    kmean = kb.astype(jnp.float32).mean(axis=3)
    gate = jnp.einsum('bhsd,bhnd->bhsn', q.astype(jnp.float32), kmean)
    qblk = jnp.arange(Sp) // BLK
    past = jnp.arange(nblk)[None, :] < qblk[:, None]
    gate = jnp.where(past, gate, -jnp.inf)
    topk = min(MOBA_TOPK, nblk)
    _, idx = lax.top_k(gate, topk)
    valid = idx < qblk[:, None]
    nc = Sp // QC
    scale = hd ** -0.5

    def chunks(t):
        return jnp.moveaxis(t.reshape((B, H, nc, QC) + t.shape[3:]), 2, 0)

    bi = jnp.arange(B)[:, None, None, None]
    hi = jnp.arange(H)[None, :, None, None]

    def one_chunk(inp):
        qc, idxc, validc, c = inp
        kg = kb[bi, hi, idxc]
        vg = vb[bi, hi, idxc]
        s_sel = jnp.einsum('bhqd,bhqkjd->bhqkj', qc, kg).astype(jnp.float32) * scale
        s_sel = jnp.where(validc[..., None], s_sel, -jnp.inf).reshape(B, H, QC, topk * BLK)
        own = (c * QC) // BLK
        ko = lax.dynamic_index_in_dim(kb, own, axis=2, keepdims=False)
        vo = lax.dynamic_index_in_dim(vb, own, axis=2, keepdims=False)
        s_own = jnp.einsum('bhqd,bhjd->bhqj', qc, ko).astype(jnp.float32) * scale
        qpos = c * QC + jnp.arange(QC)
        kpos = own * BLK + jnp.arange(BLK)
        s_own = jnp.where(kpos[None, :] <= qpos[:, None], s_own, -jnp.inf)
        p = jax.nn.softmax(jnp.concatenate([s_sel, s_own], axis=-1), axis=-1).astype(v.dtype)
        p_sel = p[..., :topk * BLK].reshape(B, H, QC, topk, BLK)
        p_own = p[..., topk * BLK:]
        return (jnp.einsum('bhqkj,bhqkjd->bhqd', p_sel, vg)
                + jnp.einsum('bhqj,bhjd->bhqd', p_own, vo))

    o = lax.map(one_chunk, (chunks(q), chunks(idx), chunks(valid), jnp.arange(nc)))
    o = o.transpose(1, 0, 3, 2, 4).reshape(B, Sp, H * hd)
    return o[:, :S]


def _hybrid_layer(x, w_in, gla_w_up, gla_b, gla_norm_g, swa_sinks, b_merge, w_branch, w_o, ln_g, ln_b):
    B, S, D = x.shape
    proj = x @ w_in
    split_idx = np.cumsum(IN_SPLITS)[:-1].tolist()
    (aq, ak, av, alr, agate, bq, bk, bv, bgate, cq, ck, cv, cgate, mgate) = jnp.split(proj, split_idx, axis=-1)
    g = jax.nn.log_sigmoid((alr @ gla_w_up + gla_b).astype(jnp.float32)) / GLA_TAU
    ya = _gla_branch(aq.reshape(B, S, GLA_HEADS, GLA_DK), ak.reshape(B, S, GLA_HEADS, GLA_DK),
                     av.reshape(B, S, GLA_HEADS, GLA_DV), g.reshape(B, S, GLA_HEADS, GLA_DK), gla_norm_g)
    ya = ya * jax.nn.silu(agate)
    yb = _swa_branch(bq.reshape(B, S, SWA_Q_HEADS, SWA_HEAD_DIM), bk.reshape(B, S, SWA_KV_HEADS, SWA_HEAD_DIM),
                     bv.reshape(B, S, SWA_KV_HEADS, SWA_HEAD_DIM), swa_sinks)
    yb = yb * jax.nn.silu(bgate)
    yc = _moba_branch(cq.reshape(B, S, MOBA_HEADS, MOBA_HEAD_DIM), ck.reshape(B, S, MOBA_HEADS, MOBA_HEAD_DIM),
                      cv.reshape(B, S, MOBA_HEADS, MOBA_HEAD_DIM))
    yc = yc * jax.nn.silu(cgate)
    ys = jnp.stack([ya.astype(x.dtype), yb.astype(x.dtype), yc.astype(x.dtype)], axis=2)
    up = jnp.einsum('bsnw,nwd->bsnd', ys, w_branch)
    gates = jax.nn.sigmoid(mgate.reshape(B, S, N_BRANCH, D) + b_merge)
    merged = jnp.sum(gates * up, axis=2)
    out = merged @ w_o
    return _layer_norm(DEEPNORM_ALPHA * x + out, ln_g, ln_b)


def setup_inputs(seed: int = 0) -> dict:
    key = jax.random.key(seed)
    ks = jax.random.split(key, 12)
    L, D = DEPTH, D_MODEL
    col_scale = np.concatenate([np.full((n,), DEEPNORM_BETA if i in VALUE_SEGMENTS else 1.0, np.float32)
                                for i, n in enumerate(IN_SPLITS)])
    x = jax.random.normal(ks[0], (BATCH, SEQ, D), jnp.float32)
    w_in = jax.random.normal(ks[1], (L, D, D_IN), jnp.float32) * (D ** -0.5) * jnp.asarray(col_scale)
    gla_w_up = jax.random.normal(ks[2], (L, GLA_RANK, GLA_HEADS * GLA_DK), jnp.float32) * (GLA_RANK ** -0.5)
    gla_b = 0.1 * jax.random.normal(ks[3], (L, GLA_HEADS * GLA_DK), jnp.float32)
    gla_norm_g = 1.0 + 0.02 * jax.random.normal(ks[4], (L, GLA_DV), jnp.float32)
    swa_sinks = 0.5 * jax.random.normal(ks[5], (L, SWA_Q_HEADS), jnp.float32)
    b_merge = 0.02 * jax.random.normal(ks[6], (L, N_BRANCH, D), jnp.float32)
    w_branch = jax.random.normal(ks[7], (L, N_BRANCH, BRANCH_WIDTH, D), jnp.float32) * (BRANCH_WIDTH ** -0.5) * DEEPNORM_BETA
    w_o = jax.random.normal(ks[8], (L, D, D), jnp.float32) * (D ** -0.5) * DEEPNORM_BETA
    ln_g = 1.0 + 0.02 * jax.random.normal(ks[9], (L, D), jnp.float32)
    ln_b = 0.02 * jax.random.normal(ks[10], (L, D), jnp.float32)
    return {"x": x, "w_in": w_in, "gla_w_up": gla_w_up, "gla_b": gla_b, "gla_norm_g": gla_norm_g,
            "swa_sinks": swa_sinks, "b_merge": b_merge, "w_branch": w_branch, "w_o": w_o,
            "ln_g": ln_g, "ln_b": ln_b}


def reference(x, w_in, gla_w_up, gla_b, gla_norm_g, swa_sinks, b_merge, w_branch, w_o, ln_g, ln_b):
    for l in range(DEPTH):
        x = _hybrid_layer(x, w_in[l], gla_w_up[l], gla_b[l], gla_norm_g[l], swa_sinks[l], b_merge[l],
                          w_branch[l], w_o[l], ln_g[l], ln_b[l])
    return x
```

```python
import math
from contextlib import ExitStack
import numpy as np
import ml_dtypes
import concourse.bass as bass
import concourse.mybir as mybir
from concourse.bass_utils import run_bass_kernel_spmd

F32 = mybir.dt.float32
BF16 = mybir.dt.bfloat16
AF = mybir.ActivationFunctionType
ALU = mybir.AluOpType

DEPTH = 4
D = 2048
T = 1024
NT = 8
D_IN = 15632
O_AQ, O_AK, O_AV, O_ALR, O_AG = 0, 512, 1024, 2048, 2064
O_BQ, O_BK, O_BV, O_BG = 3088, 4112, 4240, 4368
O_CQ, O_CK, O_CV, O_CG = 5392, 6416, 7440, 8464
O_MG = 9488
ALPHA = (2 * DEPTH) ** 0.25
LN_EPS = 1e-5
RMS_EPS = 1e-6
NEG = -30000.0

CF_U, CF_SL, CF_ONES, CF_ID, CF_U4, CF_PB, CF_RK = 0, 128, 256, 384, 512, 1024, 1152
NCF = 1160
CB_ID, CB_ONES, CB_CB, CB_Z, CB_E, CB_BAND, CB_OLO, CB_OHI, CB_OLOH, CB_OHIH = 0, 128, 256, 384, 512, 2560, 3072, 3200, 3328, 3456
NCB = 3584
P_GLAB, P_NG, P_SINK, P_BM, P_LG, P_LB = 0, 512, 514, 522, 570, 586
PW = 602


class Buf:
    def __init__(self, t, name):
        self.t = t
        self.name = name
        self.w = {}
        self.r = {}


class FW:
    def __init__(self, nc, es):
        self.nc = nc
        self.es = es
        self.eng = {'pe': nc.tensor, 'act': nc.scalar, 'dve': nc.vector, 'pool': nc.gpsimd, 'sp': nc.sync}
        self.sem = {e: es.enter_context(nc.semaphore('s_' + e)) for e in self.eng}
        self.cnt = {e: 0 for e in self.eng}
        self.waited = {e: {} for e in self.eng}
        self.pend = {e: [] for e in self.eng}
        self.dsem = {}
        self.dcnt = {}
        self.ncc = 0

    def _wait(self, e, toks):
        for key, (sem, val) in toks.items():
            if e == 'pe' and key == 'pe':
                continue
            if self.waited[e].get(key, 0) < val:
                self.eng[e].wait_ge(sem, val)
                self.waited[e][key] = val

    @staticmethod
    def _merge(d, toks):
        for k, (sem, val) in toks.items():
            if k not in d or d[k][1] < val:
                d[k] = (sem, val)

    def op(self, e, fn, reads=(), writes=(), inc=True, nws=False):
        toks = {}
        for b in reads:
            self._merge(toks, b.w)
        for b in writes:
            self._merge(toks, b.w)
            self._merge(toks, b.r)
        if nws:
            toks.pop(e, None)
        self._wait(e, toks)
        inst = fn(self.eng[e])
        if not inc:
            self.pend[e].append((reads, writes))
            return inst
        self.cnt[e] += 1
        inst.then_inc(self.sem[e], 1)
        tok = {e: (self.sem[e], self.cnt[e])}
        allrw = self.pend[e] + [(reads, writes)]
        self.pend[e] = []
        for rd, wr in allrw:
            for b in rd:
                self._merge(b.r, tok)
            for b in wr:
                b.w = dict(tok)
                b.r = {}
        return inst

    def _dsem(self, b):
        if b.name not in self.dsem:
            self.dsem[b.name] = self.es.enter_context(self.nc.semaphore('d_' + b.name))
            self.dcnt[b.name] = 0
        return self.dsem[b.name]

    def dma(self, q, out_ap, in_ap, src, dst, sbside, add=False):
        toks = {}
        self._merge(toks, src.w)
        self._merge(toks, dst.r)
        if not add:
            self._merge(toks, dst.w)
        self._wait(q, toks)
        sem = self._dsem(sbside)
        inst = self.eng[q].dma_start(out=out_ap, in_=in_ap)
        self.dcnt[sbside.name] += 16
        inst.then_inc(sem, 16)
        tok = {'d:' + sbside.name: (sem, self.dcnt[sbside.name])}
        self._merge(src.r, tok)
        if add:
            self._merge(dst.w, tok)
        else:
            dst.w = dict(tok)
        dst.r = {} if not add else dst.r
        return inst

    def allgather(self, src, dst):
        toks = {}
        self._merge(toks, src.w)
        self._merge(toks, dst.r)
        self._merge(toks, dst.w)
        self._wait('pool', toks)
        self.ncc += 1
        sem = self.es.enter_context(self.nc.semaphore('cc%d' % self.ncc))
        inst = self.nc.gpsimd.collective_compute(
            "AllGather", ALU.bypass, replica_groups=[[0, 1, 2, 3], [4, 5, 6, 7]],
            ins=[src.t.ap().opt()], outs=[dst.t.ap().opt()], dma_qos="P3")
        inst.then_inc(sem)
        tok = {'cc%d' % self.ncc: (sem, 1)}
        self._merge(src.r, tok)
        dst.w = dict(tok)
        dst.r = {}

    def finish(self, bufs):
        toks = {}
        for b in bufs:
            self._merge(toks, b.w)
        self._wait('sp', toks)

    def barrier(self):
        toks = {}
        for e in self.eng:
            if self.cnt[e] > 0:
                toks[e] = (self.sem[e], self.cnt[e])
        for name, sem in self.dsem.items():
            if self.dcnt[name] > 0:
                toks['d:' + name] = (sem, self.dcnt[name])
        for e in self.eng:
            self._wait(e, {k: v for k, v in toks.items() if k != e})


def build(nlayers=DEPTH, debug=(), phases="pxgsmo", wl=DEPTH):
    nc = bass.Bass("TRN2", target_bir_lowering=False)
    es = ExitStack()
    fw = FW(nc, es)

    def dram_in(name, shape, dt):
        return nc.dram_tensor(name, shape, dt, kind="ExternalInput")

    xT_in = Buf(dram_in("xT", [D, T], F32), "xT")
    w_in = Buf(dram_in("w_in", [wl, D, D_IN], F32), "w_in")
    w_up = Buf(dram_in("w_up", [wl, 16, 512], F32), "w_up")
    w_br = Buf(dram_in("w_br", [wl, 3, 1024, D], F32), "w_br")
    w_o = Buf(dram_in("w_o", [wl, D, D], F32), "w_o")
    prm_d = Buf(dram_in("prm", [128, DEPTH * PW], F32), "prm")
    cf_d = Buf(dram_in("cf", [128, NCF], F32), "cf")
    cb_d = Buf(dram_in("cb", [128, NCB], BF16), "cb")
    y_out = Buf(nc.dram_tensor("yT", [D, T], F32, kind="ExternalOutput"), "yT")

    def scratch(name, shape, dt):
        kind = "ExternalOutput" if name in debug else None
        if kind:
            return Buf(nc.dram_tensor(name, shape, dt, kind=kind), name)
        return Buf(nc.dram_tensor(name, shape, dt), name)

    qTa = scratch("qTa", [512, T], F32)
    kTa = scratch("kTa", [512, T], F32)
    ka = scratch("ka", [T, 512], F32)
    va = scratch("va", [T, 1024], BF16)
    lrT = scratch("lrT", [16, T], BF16)
    gTa = scratch("gTa", [1024, T], F32)
    qTb = scratch("qTb", [1024, T], BF16)
    kTb = scratch("kTb", [128, T], BF16)
    vb = scratch("vb", [T, 128], BF16)
    gTb = scratch("gTb", [1024, T], F32)
    qTc = scratch("qTc", [1024, T], BF16)
    kx = [scratch("kx%d" % j, [512, T], BF16) for j in range(2)]
    vx = [scratch("vx%d" % j, [512, 1024], BF16) for j in range(2)]
    gTc = scratch("gTc", [1024, T], F32)
    kg = [scratch("kg%d" % j, [2048, T], BF16) for j in range(2)]
    vg = [scratch("vg%d" % j, [2048, 1024], BF16) for j in range(2)]
    mx = scratch("mx", [256, 128], BF16)
    mg = scratch("mg", [1024, 128], BF16)
    sx = scratch("sx", [640, 256], F32)
    sg = scratch("sg", [2560, 256], F32)
    yTs = [scratch("yT%s" % n, [1024, T], BF16) for n in "abc"]
    xres = [scratch("xres%d" % j, [D, T], F32) for j in range(2)]

    uid = [0]

    def sbuf(stack, name, shape, dt):
        uid[0] += 1
        return Buf(stack.enter_context(nc.sbuf_tensor("%s_%d" % (name, uid[0]), shape, dt)), name)

    def group(stack, name, specs):
        b = Buf(None, name)
        b.m = {}
        for k, (shape, dt) in specs.items():
            uid[0] += 1
            b.m[k] = stack.enter_context(nc.sbuf_tensor("%s_%s_%d" % (name, k, uid[0]), shape, dt))
        return b

    def psum(stack, name, shape, dt=F32):
        uid[0] += 1
        return Buf(stack.enter_context(nc.psum_tensor("%s_%d" % (name, uid[0]), shape, dt)), name)

    xb = [sbuf(es, "xb%d" % g, [128, 16, 512], BF16) for g in range(2)]
    cf = sbuf(es, "cfs", [128, NCF], F32)
    cb = sbuf(es, "cbs", [128, NCB], BF16)
    prm = sbuf(es, "prms", [128, DEPTH * PW], F32)
    esink = sbuf(es, "esink", [128, DEPTH * 8], F32)

    fw.dma('sp', cf.t[:, :], cf_d.t.ap()[:, :], cf_d, cf, cf)
    fw.dma('sp', cb.t[:, :], cb_d.t.ap()[:, :], cb_d, cb, cb)
    fw.dma('sp', prm.t[:, :], prm_d.t.ap()[:, :], prm_d, prm, prm)
    xTv = xT_in.t.ap().rearrange("(kc p) t -> p kc t", p=128)
    for g in range(2):
        for kh in range(2):
            fw.dma('pool', xb[g].t[:, kh * 8:(kh + 1) * 8, :], xTv[:, kh * 8:(kh + 1) * 8, g * 512:(g + 1) * 512],
                   xT_in, xb[g], xb[g], add=(kh > 0))
    for l in range(DEPTH):
        fw.op('act', lambda e, l=l: e.activation(out=esink.t[:, l * 8:(l + 1) * 8],
                                                 in_=prm.t[:, l * PW + P_SINK:l * PW + P_SINK + 8], func=AF.Exp),
              reads=[prm], writes=[esink])

    PS = [psum(es, "bank%d" % i, [128, 512]) for i in range(8)]

    def cfa(off, n=128):
        return cf.t[:, off:off + n]

    def cba(off, n=128):
        return cb.t[:, off:off + n]

    def projection(l):
        with ExitStack() as ph:
            wblk = [sbuf(ph, "wblk%d" % i, [128, 16, 512], BF16) for i in range(2)]
            stf = [sbuf(ph, "stf%d" % i, [128, 512], F32) for i in range(3)]
            stb = [sbuf(ph, "stb%d" % i, [128, 512], BF16) for i in range(3)]
            ps = PS[0:4]
            st = {'n': 0, 'p': 0, 'w': 0, 'e': 0}
            wv = w_in.t.ap()

            def load_w(pieces):
                wb = wblk[st['w'] % 2]
                st['w'] += 1
                first = True
                for (dc, sc, n) in pieces:
                    src = wv[l, :, sc:sc + n].rearrange("(kc p) n -> p kc n", p=128)
                    for kh in range(2):
                        fw.dma('pool', wb.t[:, kh * 8:(kh + 1) * 8, dc:dc + n], src[:, kh * 8:(kh + 1) * 8, :],
                               w_in, wb, wb, add=not first)
                        first = False
                return wb

            def evac(pb, pap, np_, ncol, dt, func, dst, dst_ap, src_view=None):
                pool = stf if dt == F32 else stb
                sb_ = pool[st['n'] % 3]
                st['n'] += 1
                use_act = (func is not None) or (st['e'] % 2 == 0)
                st['e'] += 1
                if use_act:
                    fw.op('act', lambda e: e.activation(out=sb_.t[0:np_, 0:ncol], in_=pap,
                                                        func=(func if func is not None else AF.Copy)),
                          reads=[pb], writes=[sb_])
                else:
                    fw.op('dve', lambda e: e.tensor_copy(out=sb_.t[0:np_, 0:ncol], in_=pap), reads=[pb], writes=[sb_])
                sv = sb_.t[0:np_, 0:ncol]
                fw.dma('sp', dst_ap, (src_view(sv) if src_view else sv), sb_, dst, sb_, add=True)

            def fm(pieces, chunks, dt, func, dst, dst_row):
                wb = load_w(pieces)
                for ci, (mk, m) in enumerate(chunks):
                    for g in range(2):
                        pb = ps[st['p'] % 4]
                        st['p'] += 1
                        for kc in range(16):
                            fw.op('pe', lambda e, kc=kc: e.matmul(pb.t[0:m, :], lhsT=mk(wb, kc), rhs=xb[g].t[:, kc, :],
                                                                  start=(kc == 0), stop=(kc == 15)),
                                  reads=[wb, xb[g]], writes=[pb], inc=(kc == 15))
                        r0 = dst_row(ci)
                        evac(pb, pb.t[0:m, :], m, 512, dt, func, dst, dst.t.ap()[r0:r0 + m, g * 512:(g + 1) * 512])

            def tm(pieces, ncol, dt, dst, dst_ap_fn, src_view=None):
                wb = load_w(pieces)
                for t in range(NT):
                    g, tt = t // 4, t % 4
                    pb = ps[st['p'] % 4]
                    st['p'] += 1
                    for kc in range(16):
                        fw.op('pe', lambda e, kc=kc: e.matmul(pb.t[:, 0:ncol], lhsT=xb[g].t[:, kc, tt * 128:(tt + 1) * 128],
                                                              rhs=wb.t[:, kc, 0:ncol], start=(kc == 0), stop=(kc == 15)),
                              reads=[wb, xb[g]], writes=[pb], inc=(kc == 15))
                    evac(pb, pb.t[:, 0:ncol], 128, ncol, dt, None, dst, dst_ap_fn(t), src_view)

            def plain(n):
                return [((lambda wb, kc, c=c: wb.t[:, kc, c * 128:(c + 1) * 128]), 128) for c in range(n)]

            def perm_pieces(base, j):
                pcs = []
                for ci in range(4):
                    pcs.append((ci * 128, base + (4 * j + ci) * 64, 64))
                    pcs.append((ci * 128 + 64, base + (8 + 4 * j + ci) * 64, 64))
                return pcs

            for j in range(2):
                fm([(0, O_CK + j * 512, 512)], plain(4), BF16, None, kx[j], lambda ci: ci * 128)
            for j in range(2):
                tm([(0, O_CV + j * 512, 512)], 512, BF16, vx[j],
                   lambda t, j=j: vx[j].t.ap().rearrange("(h p) (t e) -> p t h e", p=128, e=128)[:, t, :, :],
                   src_view=lambda a: a.rearrange("p (h e) -> p h e", h=4))
            fm([(0, O_BK, 128)], plain(1), BF16, None, kTb, lambda ci: 0)
            tm([(0, O_BV, 128)], 128, BF16, vb, lambda t: vb.t.ap()[t * 128:(t + 1) * 128, :])
            tm([(0, O_AK, 512)], 512, F32, ka, lambda t: ka.t.ap()[t * 128:(t + 1) * 128, :])
            for j in range(2):
                tm([(0, O_AV + j * 512, 512)], 512, BF16, va,
                   lambda t, j=j: va.t.ap()[t * 128:(t + 1) * 128, j * 512:(j + 1) * 512])
            fm([(0, O_ALR, 16)], [((lambda wb, kc: wb.t[:, kc, 0:16]), 16)], BF16, None, lrT, lambda ci: 0)
            fm([(0, O_AK, 512)], plain(4), F32, None, kTa, lambda ci: ci * 128)
            fm([(0, O_AQ, 512)], plain(4), F32, None, qTa, lambda ci: ci * 128)
            for j in range(2):
                fm([(0, O_AG + j * 512, 512)], plain(4), F32, AF.Silu, gTa, lambda ci, j=j: j * 512 + ci * 128)
            for j in range(2):
                fm(perm_pieces(O_BQ, j), plain(4), BF16, None, qTb,
                   lambda ci, j=j: (j * 4 + ci) * 128)
            for j in range(2):
                fm(perm_pieces(O_BG, j), plain(4), F32, AF.Silu, gTb,
                   lambda ci, j=j: (j * 4 + ci) * 128)
            for j in range(2):
                fm([(0, O_CQ + j * 512, 512)], plain(4), BF16, None, qTc, lambda ci, j=j: j * 512 + ci * 128)
            for j in range(2):
                fm([(0, O_CG + j * 512, 512)], plain(4), F32, AF.Silu, gTc, lambda ci, j=j: j * 512 + ci * 128)
            exchange_kv()
            fw.barrier()

    def exchange_kv():
        with ExitStack() as ph:
            hst = sbuf(ph, "hst", [128, 256], BF16)
            fw.dma('sp', hst.t[:, 0:128], kTb.t.ap()[:, T - 128:T], kTb, hst, hst)
            fw.dma('sp', hst.t[:, 128:256], vb.t.ap()[T - 128:T, :], vb, hst, hst, add=True)
            fw.dma('sp', mx.t.ap()[0:128, :], hst.t[:, 0:128], hst, mx, hst)
            fw.dma('sp', mx.t.ap()[128:256, :], hst.t[:, 128:256], hst, mx, hst, add=True)
            fw.allgather(mx, mg)

    def gla(l):
        pc = l * PW
        with ExitStack() as ph:
            wupf = sbuf(ph, "wupf", [16, 512], BF16)
            lrs = sbuf(ph, "lrs", [16, T], BF16)
            qtl = sbuf(ph, "qtl", [128, 4, T], BF16)
            ktl = sbuf(ph, "ktl", [128, 4, T], BF16)
            khat = sbuf(ph, "khat", [128, NT, 512], BF16)
            vsb = sbuf(ph, "vsb", [128, NT, 1024], BF16)
            dl = sbuf(ph, "dl", [128, NT, 4], F32)
            dtot = sbuf(ph, "dtot", [128, 4], F32)
            S = sbuf(ph, "S", [128, 4, 256], F32)
            Sb = sbuf(ph, "Sb", [128, 4, 256], BF16)
            gstg = [group(ph, "gstg%d" % i, {'q': ([128, 4, 128], F32), 'k': ([128, 4, 128], F32), 'kt': ([128, 512], F32)})
                    for i in range(2)]
            zb2 = [sbuf(ph, "zb%d" % i, [128, 512], F32) for i in range(2)]
            ex2 = [sbuf(ph, "ex%d" % i, [128, 512], F32) for i in range(2)]
            lt2 = [sbuf(ph, "lt%d" % i, [128, 512], F32) for i in range(2)]
            pz2 = [PS[0], PS[6]]
            E1s = [sbuf(ph, "E1_%d" % i, [128, 512], F32) for i in range(2)]
            E2s = [sbuf(ph, "E2_%d" % i, [128, 512], F32) for i in range(2)]
            E3s = [sbuf(ph, "E3_%d" % i, [128, 512], F32) for i in range(2)]
            pz, pcb, pr, pat = PS[0], PS[1], PS[2], PS[5]
            psu = [PS[3], PS[4]]
            po = [PS[6], PS[7], PS[1], PS[2]]

            fw.dma('pool', wupf.t[:, :], w_up.t.ap()[l, :, :], w_up, wupf, wupf)
            fw.dma('sp', lrs.t[:, :], lrT.t.ap()[:, :], lrT, lrs, lrs)
            fw.dma('sp', vsb.t[:, :, :], va.t.ap().rearrange("(t p) c -> p t c", p=128), va, vsb, vsb)
            qv = qTa.t.ap().rearrange("(h d) t -> d h t", d=128)
            kv = kTa.t.ap().rearrange("(h d) t -> d h t", d=128)
            def stepA1(t):
                ts = slice(t * 128, (t + 1) * 128)
                G_ = gstg[t % 2]
                zb, ex, lt, pz = zb2[t % 2], ex2[t % 2], lt2[t % 2], pz2[t % 2]
                fw.dma('sp', G_.m['q'][:, :, :], qv[:, :, ts], qTa, G_, G_)
                fw.dma('sp', G_.m['k'][:, :, :], kv[:, :, ts], kTa, G_, G_, add=True)
                fw.dma('sp', G_.m['kt'][:, :], ka.t.ap()[ts, :], ka, G_, G_, add=True)
                fw.op('pe', lambda e: e.matmul(pz.t[:, :], lhsT=lrs.t[:, ts], rhs=wupf.t[:, :], start=True, stop=True),
                      reads=[lrs, wupf], writes=[pz])
                fw.op('dve', lambda e: e.tensor_tensor(out=zb.t[:, :], in0=pz.t[:, :], in1=prm.t[:, pc + P_GLAB:pc + P_GLAB + 512],
                                                       op=ALU.add), reads=[pz, prm], writes=[zb])
                fw.op('act', lambda e: e.activation(out=ex.t[:, :], in_=zb.t[:, :], func=AF.Exp, scale=-1.0),
                      reads=[zb], writes=[ex])
                fw.op('act', lambda e: e.activation(out=lt.t[:, :], in_=ex.t[:, :], func=AF.Ln, bias=1.0),
                      reads=[ex], writes=[lt])

            def stepA2(t):
                ts = slice(t * 128, (t + 1) * 128)
                G_ = gstg[t % 2]
                lt = lt2[t % 2]
                E1, E2, E3 = E1s[t % 2], E2s[t % 2], E3s[t % 2]
                for h in range(4):
                    fw.op('pe', lambda e, h=h: e.matmul(pcb.t[:, h * 128:(h + 1) * 128], lhsT=lt.t[:, h * 128:(h + 1) * 128],
                                                        rhs=cfa(CF_U), start=True, stop=True),
                          reads=[lt, cf], writes=[pcb], inc=(h == 3))
                fw.op('pe', lambda e: e.matmul(pr.t[:, :], lhsT=cfa(CF_SL), rhs=lt.t[:, :], start=True, stop=True),
                      reads=[lt, cf], writes=[pr])
                fw.op('act', lambda e: e.activation(out=E1.t[:, :], in_=pcb.t[:, :], func=AF.Exp, scale=-1.0 / 16),
                      reads=[pcb], writes=[E1])
                fw.op('act', lambda e: e.activation(out=E2.t[:, :], in_=pcb.t[:, :], func=AF.Exp, scale=1.0 / 16),
                      reads=[pcb], writes=[E2])
                fw.op('act', lambda e: e.activation(out=E3.t[:, :], in_=pr.t[:, :], func=AF.Exp, scale=-1.0 / 16),
                      reads=[pr], writes=[E3])
                E1v = E1.t[:, :].rearrange("p (h i) -> p h i", h=4)
                E2v = E2.t[:, :].rearrange("p (h i) -> p h i", h=4)
                fw.op('dve', lambda e: e.scalar_tensor_tensor(out=qtl.t[:, :, ts], in0=G_.m['q'][:, :, :], scalar=128 ** -0.5,
                                                              in1=E1v, op0=ALU.mult, op1=ALU.mult),
                      reads=[G_, E1], writes=[qtl], nws=True)
                fw.op('dve', lambda e: e.tensor_tensor(out=ktl.t[:, :, ts], in0=G_.m['k'][:, :, :], in1=E2v, op=ALU.mult),
                      reads=[G_, E2], writes=[ktl], nws=True)
                fw.op('dve', lambda e: e.tensor_tensor(out=khat.t[:, t, :], in0=G_.m['kt'][:, :], in1=E3.t[:, :], op=ALU.mult),
                      reads=[G_, E3], writes=[khat], nws=True)
                fw.op('dve', lambda e: e.tensor_copy(out=dl.t[:, t, :], in_=E1v[:, :, 127]), reads=[E1], writes=[dl], nws=True)

            stepA1(0)
            for t in range(NT):
                if t + 1 < NT:
                    stepA1(t + 1)
                stepA2(t)

            def state_update(t, with_bf):
                for h in range(4):
                    pb = psu[h // 2]
                    fw.op('pe', lambda e, h=h, pb=pb: e.matmul(pb.t[:, (h % 2) * 256:(h % 2 + 1) * 256],
                                                               lhsT=khat.t[:, t, h * 128:(h + 1) * 128],
                                                               rhs=vsb.t[:, t, h * 256:(h + 1) * 256], start=True, stop=True),
                          reads=[khat, vsb], writes=[pb], inc=(h % 2 == 1))
                for h in range(4):
                    pb = psu[h // 2]
                    fw.op('dve', lambda e, h=h, pb=pb: e.scalar_tensor_tensor(
                        out=S.t[:, h, :], in0=S.t[:, h, :], scalar=dl.t[:, t, h:h + 1],
                        in1=pb.t[:, (h % 2) * 256:(h % 2 + 1) * 256], op0=ALU.mult, op1=ALU.add),
                        reads=[S, dl, pb], writes=[S], nws=(h > 0))
                if with_bf:
                    fw.op('act', lambda e: e.activation(out=Sb.t[:, :, :], in_=S.t[:, :, :], func=AF.Copy),
                          reads=[S], writes=[Sb])

            fw.op('dve', lambda e: e.memset(S.t[:, :, :], 0.0), writes=[S])
            fw.op('dve', lambda e: e.tensor_copy(out=dtot.t[:, :], in_=dl.t[:, 0, :]), reads=[dl], writes=[dtot])
            for t in range(1, NT):
                fw.op('dve', lambda e, t=t: e.tensor_tensor(out=dtot.t[:, :], in0=dtot.t[:, :], in1=dl.t[:, t, :], op=ALU.mult),
                      reads=[dl, dtot], writes=[dtot])
            for t in range(NT):
                state_update(t, False)
            zpad = sbuf(ph, "zpad", [128, 256], F32)
            fw.op('dve', lambda e: e.memset(zpad.t[:, :], 0.0), writes=[zpad])
            fw.op('dve', lambda e: e.tensor_copy(out=zpad.t[:, 0:4], in_=dtot.t[:, :]), reads=[dtot], writes=[zpad])
            fw.dma('sp', sx.t.ap()[0:512, :].rearrange("(h d) e -> d h e", d=128), S.t[:, :, :], S, sx, S)
            fw.dma('sp', sx.t.ap()[512:640, :], zpad.t[:, :], zpad, sx, zpad, add=True)
            fw.allgather(sx, sg)
            yield
            fw.allgather(kx[0], kg[0])
            fw.allgather(vx[0], vg[0])
            SL = sbuf(ph, "SLg", [128, 4, 4, 256], F32)
            Dg = sbuf(ph, "Dg", [128, 4, 256], F32)
            u = sbuf(ph, "u_", [128, 256], F32)
            sgv = sg.t.ap().rearrange("(r x) e -> x r e", x=640)
            for r in range(4):
                fw.dma('sp', SL.t[:, r, :, :], sgv[0:512, r, :].rearrange("(h d) e -> d h e", d=128), sg, SL, SL, add=(r > 0))
            fw.dma('sp', Dg.t[:, :, :], sgv[512:640, :, :], sg, Dg, Dg)
            fw.op('dve', lambda e: e.memset(S.t[:, :, :], 0.0), writes=[S])
            for r in range(3):
                for h in range(4):
                    fw.op('dve', lambda e, r=r, h=h: e.scalar_tensor_tensor(
                        out=u.t[:, :], in0=S.t[:, h, :], scalar=Dg.t[:, r, h:h + 1], in1=SL.t[:, r, h, :],
                        op0=ALU.mult, op1=ALU.add), reads=[S, Dg, SL], writes=[u])
                    fw.op('dve', lambda e, h=h: e.tensor_tensor(out=u.t[:, :], in0=u.t[:, :], in1=S.t[:, h, :], op=ALU.subtract),
                          reads=[u, S], writes=[u])
                    fw.op('dve', lambda e, r=r, h=h: e.scalar_tensor_tensor(
                        out=S.t[:, h, :], in0=u.t[:, :], scalar=cf.t[:, CF_RK + r:CF_RK + r + 1], in1=S.t[:, h, :],
                        op0=ALU.mult, op1=ALU.add), reads=[u, S, cf], writes=[S])
            fw.op('act', lambda e: e.activation(out=Sb.t[:, :, :], in_=S.t[:, :, :], func=AF.Copy), reads=[S], writes=[Sb])
            amb = sbuf(ph, "amb", [128, 512], BF16)
            sq = [sbuf(ph, "sq%d" % i, [128, 512], F32) for i in range(2)]
            sd = sbuf(ph, "sd", [128, 512], F32)
            rs = sbuf(ph, "rs", [128, 512], F32)
            t1 = sbuf(ph, "t1", [128, 8, 128], F32)
            gst = [sbuf(ph, "gst%d" % i, [128, 8, 128], F32) for i in range(2)]
            yst = [sbuf(ph, "yst%d" % i, [128, 8, 128], BF16) for i in range(2)]
            pss = pz
            gv = gTa.t.ap().rearrange("(c e) t -> e c t", e=128)
            yv = yTs[0].t.ap().rearrange("(c e) t -> e c t", e=128)
            def crit(t):
                ts = slice(t * 128, (t + 1) * 128)
                g_ = gst[t % 2]
                y_ = yst[t % 2]
                fw.dma('sp', g_.t[:, :, :], gv[:, :, ts], gTa, g_, g_)
                for h in range(4):
                    fw.op('pe', lambda e, h=h: e.matmul(pat.t[:, h * 128:(h + 1) * 128], lhsT=ktl.t[:, h, ts], rhs=qtl.t[:, h, ts],
                                                        start=True, stop=True), reads=[ktl, qtl], writes=[pat], inc=(h == 3))
                fw.op('dve', lambda e: e.tensor_tensor(out=amb.t[:, :], in0=pat.t[:, :], in1=cf.t[:, CF_U4:CF_U4 + 512], op=ALU.mult),
                      reads=[pat, cf], writes=[amb])
                pob = [po[(t % 2) * 2], po[(t % 2) * 2 + 1]]
                for h in range(4):
                    pb = pob[h // 2]
                    for ec in range(2):
                        col = ((h % 2) * 2 + ec) * 128
                        fw.op('pe', lambda e, h=h, ec=ec, pb=pb, col=col: e.matmul(
                            pb.t[:, col:col + 128], lhsT=Sb.t[:, h, ec * 128:(ec + 1) * 128], rhs=qtl.t[:, h, ts],
                            start=True, stop=False), reads=[Sb, qtl], writes=[pb], inc=False)
                        fw.op('pe', lambda e, h=h, ec=ec, pb=pb, col=col: e.matmul(
                            pb.t[:, col:col + 128], lhsT=vsb.t[:, t, h * 256 + ec * 128:h * 256 + (ec + 1) * 128],
                            rhs=amb.t[:, h * 128:(h + 1) * 128], start=False, stop=True),
                            reads=[vsb, amb], writes=[pb], inc=(h % 2 == 1 and ec == 1))
                state_update(t, True)

            def outp(t):
                ts = slice(t * 128, (t + 1) * 128)
                g_ = gst[t % 2]
                y_ = yst[t % 2]
                pob = [po[(t % 2) * 2], po[(t % 2) * 2 + 1]]
                for i in range(2):
                    fw.op('act', lambda e, i=i: e.activation(out=sq[i].t[:, :], in_=pob[i].t[:, :], func=AF.Square),
                          reads=[pob[i]], writes=[sq[i]])
                for h in range(4):
                    for ec in range(2):
                        col = ((h % 2) * 2 + ec) * 128
                        fw.op('pe', lambda e, h=h, ec=ec, col=col: e.matmul(
                            pss.t[:, h * 128:(h + 1) * 128], lhsT=cfa(CF_ONES), rhs=sq[h // 2].t[:, col:col + 128],
                            start=(ec == 0), stop=(ec == 1)), reads=[cf, sq[h // 2]], writes=[pss], inc=(h == 3 and ec == 1))
                fw.op('act', lambda e: e.activation(out=sd.t[:, :], in_=pss.t[:, :], func=AF.Ln, scale=1.0 / 256, bias=RMS_EPS),
                      reads=[pss], writes=[sd])
                fw.op('act', lambda e: e.activation(out=rs.t[:, :], in_=sd.t[:, :], func=AF.Exp, scale=-0.5), reads=[sd], writes=[rs])
                for i in range(2):
                    for ec in range(2):
                        pv = pob[i].t[:, :].rearrange("p (h c i) -> p h c i", h=2, c=2)[:, :, ec, :]
                        rv = rs.t[:, i * 256:(i + 1) * 256].rearrange("p (h i) -> p h i", h=2)
                        ov = t1.t[:, i * 4:(i + 1) * 4, :].rearrange("p (h c) i -> p h c i", h=2)[:, :, ec, :]
                        fw.op('dve', lambda e, pv=pv, rv=rv, ov=ov, ec=ec: e.scalar_tensor_tensor(
                            out=ov, in0=pv, scalar=prm.t[:, pc + P_NG + ec:pc + P_NG + ec + 1], in1=rv,
                            op0=ALU.mult, op1=ALU.mult), reads=[pob[i], rs, prm], writes=[t1], nws=(i + ec > 0))
                fw.op('dve', lambda e: e.tensor_tensor(out=y_.t[:, :, :], in0=t1.t[:, :, :], in1=g_.t[:, :, :], op=ALU.mult),
                      reads=[t1, g_], writes=[y_])
                fw.dma('sp', yv[:, :, ts], y_.t[:, :, :], y_, yTs[0], y_, add=True)

            crit(0)
            for t in range(1, NT):
                crit(t)
                outp(t - 1)
            outp(NT - 1)
            fw.barrier()

    def swa(l):
        with ExitStack() as ph:
            kts_ = sbuf(ph, "skT", [128, 9 * 128], BF16)
            vlo = sbuf(ph, "vlo", [128, 9, 128], BF16)
            vhi = sbuf(ph, "vhi", [128, 9, 128], BF16)
            hal = sbuf(ph, "hal", [128, 4, 256], BF16)
            hsum = sbuf(ph, "hsum", [128, 256], F32)
            vtmp = sbuf(ph, "vtmp", [128, 8, 128], BF16)
            sgrp = [group(ph, "sgrp%d" % i, {'q': ([128, T], BF16), 'g': ([128, T], F32)}) for i in range(2)]
            PT2 = [[sbuf(ph, "sPT%d_%d" % (j, i), [128, 512], BF16) for i in range(9)] for j in range(2)]
            den = sbuf(ph, "sden", [128, 512], F32)
            rden = sbuf(ph, "srden", [128, 512], F32)
            o1 = sbuf(ph, "so1", [128, 512], F32)
            ys = [sbuf(ph, "sys%d" % i, [128, T], BF16) for i in range(2)]
            pS = [[PS[0], PS[1]], [PS[2], PS[3]]]
            pN = [PS[4], PS[5]]
            pD = [PS[6], PS[7]]
            mgv = mg.t.ap().rearrange("(r x) c -> x r c", x=256)
            fw.dma('sp', hal.t[:, :, 0:128], mgv[0:128, :, :], mg, hal, hal)
            fw.dma('sp', hal.t[:, :, 128:256], mgv[128:256, :, :], mg, hal, hal, add=True)
            fw.op('dve', lambda e: e.tensor_scalar(out=hsum.t[:, :], in0=hal.t[:, 0, :], scalar1=cf.t[:, CF_RK + 3:CF_RK + 4],
                                                   scalar2=None, op0=ALU.mult), reads=[hal, cf], writes=[hsum])
            for r in range(1, 4):
                fw.op('dve', lambda e, r=r: e.scalar_tensor_tensor(out=hsum.t[:, :], in0=hal.t[:, r, :],
                                                                   scalar=cf.t[:, CF_RK + 3 + r:CF_RK + 4 + r],
                                                                   in1=hsum.t[:, :], op0=ALU.mult, op1=ALU.add),
                      reads=[hal, cf, hsum], writes=[hsum])
            fw.dma('sp', kts_.t[:, 128:], kTb.t.ap()[:, :], kTb, kts_, kts_)
            fw.op('dve', lambda e: e.tensor_copy(out=kts_.t[:, 0:128], in_=hsum.t[:, 0:128]), reads=[hsum, kts_], writes=[kts_])
            fw.dma('sp', vtmp.t[:, :, :], vb.t.ap().rearrange("(t p) c -> p t c", p=128), vb, vtmp, vtmp)
            fw.op('dve', lambda e: e.memset(vlo.t[:, :, :], 0.0), writes=[vlo])
            fw.op('dve', lambda e: e.memset(vhi.t[:, :, :], 0.0), writes=[vhi])
            fw.op('dve', lambda e: e.tensor_copy(out=vlo.t[:, 1:9, 0:64], in_=vtmp.t[:, :, 0:64]), reads=[vtmp, vlo], writes=[vlo])
            fw.op('dve', lambda e: e.tensor_copy(out=vhi.t[:, 1:9, 64:128], in_=vtmp.t[:, :, 64:128]), reads=[vtmp, vhi], writes=[vhi])
            fw.op('dve', lambda e: e.tensor_copy(out=vlo.t[:, 0, 0:64], in_=hsum.t[:, 128:192]), reads=[hsum, vlo], writes=[vlo])
            fw.op('dve', lambda e: e.tensor_copy(out=vhi.t[:, 0, 64:128], in_=hsum.t[:, 192:256]), reads=[hsum, vhi], writes=[vhi])
            band3 = cb.t[:, CB_BAND:CB_BAND + 512].rearrange("p (h n) -> p h n", h=2)
            scale = 64 ** -0.5
            def sw_scores(c):
                SG_, y_ = sgrp[c % 2], ys[c % 2]
                PT = PT2[c % 2]
                fw.dma('sp', SG_.m['q'][:, :], qTb.t.ap()[c * 128:(c + 1) * 128, :], qTb, SG_, SG_)
                fw.dma('sp', SG_.m['g'][:, :], gTb.t.ap()[c * 128:(c + 1) * 128, :], gTb, SG_, SG_, add=True)
                for k in range(9):
                    q0 = max(k - 1, 0)
                    q1 = min(k, 7)
                    nq = q1 - q0 + 1
                    N = nq * 128
                    for hh in range(2):
                        pb = pS[k % 2][hh]
                        fw.op('pe', lambda e, hh=hh, pb=pb: e.matmul(pb.t[:, 0:N],
                                                              lhsT=kts_.t[hh * 64:(hh + 1) * 64, k * 128:(k + 1) * 128],
                                                              rhs=SG_.m['q'][hh * 64:(hh + 1) * 64, q0 * 128:q0 * 128 + N],
                                                              start=True, stop=True), reads=[kts_, SG_], writes=[pb])
                    tv = PT[k].t[:, :].rearrange("p (h n) -> p h n", h=2)[:, :, 0:N]
                    if k == 0:
                        mv = band3[:, :, 128:256]
                    elif k == 8:
                        mv = band3[:, :, 0:128]
                    else:
                        mv = band3[:, :, 0:256]
                    for hh in range(2):
                        pb = pS[k % 2][hh]
                        fw.op('act', lambda e, hh=hh, pb=pb: e.activation(out=PT[k].t[:, hh * 256:hh * 256 + N], in_=pb.t[:, 0:N],
                                                                          func=AF.Exp, scale=scale),
                              reads=[pb], writes=[PT[k]], nws=(hh == 1))
                    fw.op('dve', lambda e, tv=tv, mv=mv: e.tensor_tensor(out=tv, in0=tv, in1=mv, op=ALU.mult),
                          reads=[PT[k], cb], writes=[PT[k]])

            def sw_pv(c):
                SG_, y_ = sgrp[c % 2], ys[c % 2]
                PT = PT2[c % 2]
                for tg in range(2):
                    pn, pd = pN[tg % 2], pD[tg % 2]
                    nmm = 0
                    for tt in range(4):
                        t = tg * 4 + tt
                        lst = []
                        for k in (t, t + 1):
                            q0 = max(k - 1, 0)
                            off = (t - q0) * 128
                            for hh in range(2):
                                lst.append((k, hh, off))
                        for idx, (k, hh, off) in enumerate(lst):
                            vsrc = vlo if hh == 0 else vhi
                            fw.op('pe', lambda e, k=k, hh=hh, off=off, vsrc=vsrc: e.matmul(
                                pn.t[:, tt * 128:(tt + 1) * 128], lhsT=vsrc.t[:, k, :], rhs=PT[k].t[:, hh * 256 + off:hh * 256 + off + 128],
                                start=(nmm == 0), stop=(nmm == 15)), reads=[vsrc, PT[k]], writes=[pn], inc=(nmm == 15))
                            nmm += 1
                    nmm = 0
                    for tt in range(4):
                        t = tg * 4 + tt
                        lst = []
                        for k in (t, t + 1):
                            q0 = max(k - 1, 0)
                            off = (t - q0) * 128
                            for hh in range(2):
                                lst.append((k, hh, off))
                        for idx, (k, hh, off) in enumerate(lst):
                            if k == 0:
                                oo = CB_OLOH if hh == 0 else CB_OHIH
                            else:
                                oo = CB_OLO if hh == 0 else CB_OHI
                            fw.op('pe', lambda e, k=k, hh=hh, off=off, oo=oo: e.matmul(
                                pd.t[:, tt * 128:(tt + 1) * 128], lhsT=cba(oo), rhs=PT[k].t[:, hh * 256 + off:hh * 256 + off + 128],
                                start=(nmm == 0), stop=(nmm == 15)), reads=[cb, PT[k]], writes=[pd], inc=(nmm == 15))
                            nmm += 1
                    gs_ = slice(tg * 512, (tg + 1) * 512)
                    fw.op('act', lambda e: e.activation(out=den.t[:, :], in_=pd.t[:, :], func=AF.Ln,
                                                        bias=esink.t[:, l * 8 + c:l * 8 + c + 1]), reads=[pd, esink], writes=[den])
                    fw.op('act', lambda e: e.activation(out=rden.t[:, :], in_=den.t[:, :], func=AF.Exp, scale=-1.0),
                          reads=[den], writes=[rden])
                    fw.op('dve', lambda e: e.tensor_tensor(out=o1.t[:, :], in0=pn.t[:, :], in1=rden.t[:, :], op=ALU.mult),
                          reads=[pn, rden], writes=[o1])
                    fw.op('dve', lambda e: e.tensor_tensor(out=y_.t[:, gs_], in0=o1.t[:, :], in1=SG_.m['g'][:, gs_], op=ALU.mult),
                          reads=[o1, SG_], writes=[y_])
                fw.dma('sp', yTs[1].t.ap()[c * 128:(c + 1) * 128, :], y_.t[:, :], y_, yTs[1], y_, add=True)

            sw_scores(0)
            for c in range(8):
                if c + 1 < 8:
                    sw_scores(c + 1)
                sw_pv(c)
            fw.barrier()

    def moba(l):
        with ExitStack() as ph:
            mgrp = [group(ph, "mgrp%d" % i, {'KT': ([128, 4, T], BF16), 'V': ([128, 4, 8, 128], BF16), 'q': ([128, T], BF16),
                                              'ko': ([128, T], BF16), 'vo': ([128, 8, 128], BF16), 'g': ([128, T], F32)})
                    for i in range(2)]
            yh = [sbuf(ph, "my%d" % i, [128, T], BF16) for i in range(2)]
            kms = sbuf(ph, "kms", [128, 16], F32)
            kmb = sbuf(ph, "kmb", [128, 16], BF16)
            gm = sbuf(ph, "gm", [128, 8, 16], F32)
            top8 = sbuf(ph, "top8", [128, 8, 8], F32)
            thr = sbuf(ph, "thr", [128, 8], F32)
            mbt = sbuf(ph, "mbt", [128, 8, 16], F32)
            mbT2 = [sbuf(ph, "mbT%d" % i, [16, T], BF16) for i in range(2)]
            PT = [sbuf(ph, "mPT%d" % i, [128, 512], BF16) for i in range(4)]
            rd = sbuf(ph, "mrd", [128, 512], F32)
            o1 = sbuf(ph, "mo1", [128, 512], F32)
            pG = PS[0]
            pTs = [PS[1], PS[2]]
            pS = [PS[3], PS[4], PS[5]]
            pN = PS[6]
            pD = PS[7]
            scale = 128 ** -0.5

            def proA(h):
                j, hh = h // 4, h % 4
                M_ = mgrp[h % 2]
                KT_t, V_t, q_t, ko_t, vo_t, g_t = (M_.m[k] for k in ('KT', 'V', 'q', 'ko', 'vo', 'g'))
                kgv = kg[j].t.ap().rearrange("(r x d) k -> d x r k", x=4, d=128)
                vgv = vg[j].t.ap().rearrange("(r x p) (t e) -> p x r t e", x=4, p=128, e=128)
                for r in range(4):
                    fw.dma('sp', KT_t[:, r, :], kgv[:, hh, r, :], kg[j], M_, M_, add=(r > 0))
                for r in range(4):
                    fw.dma('sp', V_t[:, r, :, :], vgv[:, hh, r, :, :], vg[j], M_, M_, add=True)
                fw.dma('sp', q_t[:, :], qTc.t.ap()[h * 128:(h + 1) * 128, :], qTc, M_, M_, add=True)
                fw.dma('sp', ko_t[:, :], kx[j].t.ap()[hh * 128:(hh + 1) * 128, :], kx[j], M_, M_, add=True)
                fw.dma('sp', vo_t[:, :, :], vx[j].t.ap()[hh * 128:(hh + 1) * 128, :].rearrange("p (t e) -> p t e", e=128),
                       vx[j], M_, M_, add=True)
                fw.dma('sp', g_t[:, :], gTc.t.ap()[h * 128:(h + 1) * 128, :], gTc, M_, M_, add=True)

            def proA2(h):
                M_ = mgrp[h % 2]
                KT_t = M_.m['KT']
                fw.op('dve', lambda e: e.tensor_reduce(out=kms.t[:, :], in_=KT_t[:, :, :].rearrange("p r (b k) -> p (r b) k", k=256),
                                                       axis=mybir.AxisListType.X, op=ALU.add), reads=[M_], writes=[kms])
                fw.op('dve', lambda e: e.tensor_scalar(out=kmb.t[:, :], in0=kms.t[:, :], scalar1=1.0 / 256, scalar2=None, op0=ALU.mult),
                      reads=[kms], writes=[kmb])

            def proB(h):
                M_ = mgrp[h % 2]
                q_t = M_.m['q']
                for t in range(NT):
                    fw.op('pe', lambda e, t=t: e.matmul(pG.t[:, t * 16:(t + 1) * 16], lhsT=q_t[:, t * 128:(t + 1) * 128], rhs=kmb.t[:, :],
                                                        start=True, stop=True), reads=[M_, kmb], writes=[pG], inc=(t == NT - 1))
                fw.op('dve', lambda e: e.tensor_tensor(out=gm.t[:, :, :].rearrange("p t n -> p (t n)"), in0=pG.t[:, 0:128],
                                                       in1=cf.t[:, CF_PB:CF_PB + 128], op=ALU.add), reads=[pG, cf], writes=[gm])
                for t in range(NT):
                    fw.op('dve', lambda e, t=t: e.max(out=top8.t[:, t, :], in_=gm.t[:, t, :]), reads=[gm], writes=[top8], nws=(t > 0))
                fw.op('dve', lambda e: e.tensor_scalar(out=thr.t[:, :], in0=top8.t[:, :, 2], scalar1=-1e29, scalar2=None, op0=ALU.max),
                      reads=[top8], writes=[thr])
                for t in range(NT):
                    fw.op('dve', lambda e, t=t: e.tensor_scalar(out=mbt.t[:, t, :], in0=gm.t[:, t, :], scalar1=thr.t[:, t:t + 1],
                                                                scalar2=NEG, op0=ALU.is_lt, op1=ALU.mult),
                          reads=[gm, thr], writes=[mbt], nws=(t > 0))

            def proC(h):
                mbT = mbT2[h % 2]
                for t in range(NT):
                    pt = pTs[t // 4]
                    fw.op('pe', lambda e, t=t, pt=pt: e.transpose(out=pt.t[0:16, (t % 4) * 128:(t % 4 + 1) * 128], in_=mbt.t[:, t, :],
                                                                  identity=cfa(CF_ID)),
                          reads=[mbt, cf], writes=[pt], inc=(t % 4 == 3))
                for i in range(2):
                    fw.op('act', lambda e, i=i: e.activation(out=mbT.t[:, i * 512:(i + 1) * 512], in_=pTs[i].t[0:16, :], func=AF.Copy),
                          reads=[pTs[i]], writes=[mbT])

            stc = [0]

            def main(h, hooks):
                M_, y_ = mgrp[h % 2], yh[h % 2]
                mbT = mbT2[h % 2]
                KT_t, V_t, q_t, ko_t, vo_t, g_t = (M_.m[k] for k in ('KT', 'V', 'q', 'ko', 'vo', 'g'))
                nitem = 0
                for g in range(2):
                    qs_ = slice(g * 512, (g + 1) * 512)
                    first = [True]

                    def pv_acc(ptb, vap, cs, last=False):
                        fw.op('pe', lambda e: e.matmul(pN.t[:, cs], lhsT=vap, rhs=ptb.t[:, 0:cs.stop - cs.start],
                                                       start=first[0], stop=last), reads=[ptb, M_], writes=[pN], inc=False)
                        fw.op('pe', lambda e: e.matmul(pD.t[:, cs], lhsT=cba(CB_ONES), rhs=ptb.t[:, 0:cs.stop - cs.start],
                                                       start=first[0], stop=last), reads=[ptb, cb], writes=[pD], inc=True)
                        first[0] = False

                    def stage1(kind, a):
                        pb = pS[stc[0] % 3]
                        ptb = PT[stc[0] % 4]
                        stc[0] += 1
                        if kind == 'g':
                            r, tt = a // 8, a % 8
                            n = a // 2
                            fw.op('pe', lambda e: e.matmul(pb.t[:, :], lhsT=KT_t[:, r, tt * 128:(tt + 1) * 128], rhs=q_t[:, qs_],
                                                           start=True, stop=False), reads=[M_], writes=[pb], inc=False)
                            fw.op('pe', lambda e: e.matmul(pb.t[:, :], lhsT=cb.t[0:16, CB_E + n * 128:CB_E + (n + 1) * 128],
                                                           rhs=mbT.t[:, qs_], start=False, stop=True), reads=[cb, mbT], writes=[pb])
                            fw.op('act', lambda e: e.activation(out=ptb.t[:, :], in_=pb.t[:, :], func=AF.Exp, scale=scale),
                                  reads=[pb], writes=[ptb])
                            return (ptb, V_t[:, r, tt, :], slice(0, 512))
                        elif kind == 'a':
                            ta = 2 * a
                            c0 = (ta - 4 * g) * 128
                            fw.op('pe', lambda e: e.matmul(pb.t[:, 0:256], lhsT=ko_t[:, ta * 128:(ta + 1) * 128],
                                                           rhs=q_t[:, ta * 128:(ta + 2) * 128], start=True, stop=False),
                                  reads=[M_], writes=[pb], inc=False)
                            fw.op('pe', lambda e: e.matmul(pb.t[:, 0:256], lhsT=cba(CB_ID), rhs=cb.t[:, CB_CB:CB_CB + 256],
                                                           start=False, stop=True), reads=[cb], writes=[pb])
                            fw.op('act', lambda e: e.activation(out=ptb.t[:, 0:256], in_=pb.t[:, 0:256], func=AF.Exp, scale=scale),
                                  reads=[pb], writes=[ptb])
                            return (ptb, vo_t[:, ta, :], slice(c0, c0 + 256))
                        else:
                            tb = 2 * a + 1
                            c0 = (tb - 4 * g) * 128
                            fw.op('pe', lambda e: e.matmul(pb.t[:, 0:128], lhsT=ko_t[:, tb * 128:(tb + 1) * 128],
                                                           rhs=q_t[:, tb * 128:(tb + 1) * 128], start=True, stop=False),
                                  reads=[M_], writes=[pb], inc=False)
                            fw.op('pe', lambda e: e.matmul(pb.t[:, 0:128], lhsT=cba(CB_ID), rhs=cb.t[:, CB_CB:CB_CB + 128],
                                                           start=False, stop=True), reads=[cb], writes=[pb])
                            fw.op('act', lambda e: e.activation(out=ptb.t[:, 0:128], in_=pb.t[:, 0:128], func=AF.Exp, scale=scale),
                                  reads=[pb], writes=[ptb])
                            return (ptb, vo_t[:, tb, :], slice(c0, c0 + 128))

                    items = []
                    for kt in range(28 if g == 0 else 30):
                        items.append(('g', kt))
                        if kt == 15:
                            for lb in (2 * g, 2 * g + 1):
                                items.append(('a', lb))
                                items.append(('b', lb))
                    queue = []
                    for ii, (kind, a) in enumerate(items):
                        queue.append(stage1(kind, a))
                        if len(queue) > 2:
                            pv_acc(*queue.pop(0))
                        nitem += 1
                        if nitem in hooks:
                            hooks[nitem]()
                    while queue:
                        pv = queue.pop(0)
                        pv_acc(*pv, last=(len(queue) == 0))
                    fw.op('act', lambda e: e.activation(out=o1.t[:, :], in_=pD.t[:, :], func=AF.Ln), reads=[pD], writes=[o1])
                    fw.op('act', lambda e: e.activation(out=rd.t[:, :], in_=o1.t[:, :], func=AF.Exp, scale=-1.0), reads=[o1], writes=[rd])
                    fw.op('dve', lambda e: e.tensor_tensor(out=o1.t[:, :], in0=pN.t[:, :], in1=rd.t[:, :], op=ALU.mult),
                          reads=[pN, rd], writes=[o1])
                    fw.op('dve', lambda e: e.tensor_tensor(out=y_.t[:, qs_], in0=o1.t[:, :], in1=g_t[:, qs_], op=ALU.mult),
                          reads=[o1, M_], writes=[y_])
                fw.dma('sp', yTs[2].t.ap()[h * 128:(h + 1) * 128, :], y_.t[:, :], y_, yTs[2], y_, add=True)

            proA(0)
            fw.allgather(kx[1], kg[1])
            fw.allgather(vx[1], vg[1])
            proA2(0)
            proB(0)
            proC(0)
            for h in range(8):
                hooks = {}
                if h + 1 < 8:
                    hooks = {2: (lambda h=h: proA(h + 1)), 16: (lambda h=h: proA2(h + 1)),
                             30: (lambda h=h: proB(h + 1)), 52: (lambda h=h: proC(h + 1))}
                main(h, hooks)
            fw.barrier()

    def merge_out_g(l, xsrc, xdst, glist, ysb, ys_scope, merged):
        pc = l * PW
        wv = w_in.t.ap()
        with ExitStack() as ph:
            with ExitStack() as pa:
                wset = [group(pa, "wset%d" % i, dict([('m%d' % n, ([128, 16, 128], BF16)) for n in range(3)] +
                                                      [('b%d' % n, ([128, 8, 128], BF16)) for n in range(3)])) for i in range(2)]
                sgt = [sbuf(pa, "sgt%d" % i, [128, 512], F32) for i in range(3)]
                tm_ = [sbuf(pa, "tmm%d" % i, [128, 512], F32) for i in range(2)]
                macc = sbuf(pa, "macc", [128, 512], F32)
                pm = PS[0:4]
                pu = PS[4:8]
                cnt = 0
                fw.dma('sp', ysb[2].t[:, :, :], yTs[2].t.ap().rearrange("(c p) t -> p c t", p=128), yTs[2], ysb[2], ysb[2])
                for dc in range(16):
                    W_ = wset[dc % 2]
                    firstw = True
                    for n in range(3):
                        c0 = O_MG + n * D + dc * 128
                        src = wv[l, :, c0:c0 + 128].rearrange("(kc p) n -> p kc n", p=128)
                        fw.dma('pool', W_.m['m%d' % n][:, :, :], src, w_in, W_, W_, add=not firstw)
                        firstw = False
                        if n == 1:
                            for half in range(2):
                                srcb = w_br.t.ap()[l, 1, half * 512:(half + 1) * 512, dc * 128:(dc + 1) * 128].rearrange(
                                    "(c p) n -> p c n", p=64)
                                fw.dma('pool', W_.m['b1'][half * 64:(half + 1) * 64, :, :], srcb, w_br, W_, W_, add=True)
                        else:
                            srcb = w_br.t.ap()[l, n, :, dc * 128:(dc + 1) * 128].rearrange("(c p) n -> p c n", p=128)
                            fw.dma('pool', W_.m['b%d' % n][:, :, :], srcb, w_br, W_, W_, add=True)
                    for g in glist:
                        gs_ = slice(g * 512, (g + 1) * 512)
                        for n in range(3):
                            pmb, pub = pm[cnt % 4], pu[cnt % 4]
                            cnt += 1
                            for kc in range(16):
                                fw.op('pe', lambda e, kc=kc: e.matmul(pmb.t[:, :], lhsT=W_.m['m%d' % n][:, kc, :],
                                                                      rhs=xb[g].t[:, kc, :], start=(kc == 0), stop=(kc == 15)),
                                      reads=[W_, xb[g]], writes=[pmb], inc=(kc == 15))
                            for wc in range(8):
                                fw.op('pe', lambda e, wc=wc: e.matmul(pub.t[:, :], lhsT=W_.m['b%d' % n][:, wc, :],
                                                                      rhs=ysb[n].t[:, wc, gs_], start=(wc == 0), stop=(wc == 7)),
                                      reads=[W_, ysb[n]], writes=[pub], inc=(wc == 7))
                            s_ = sgt[n]
                            fw.op('act', lambda e: e.activation(out=s_.t[:, :], in_=pmb.t[:, :], func=AF.Sigmoid,
                                                                bias=prm.t[:, pc + P_BM + n * 16 + dc:pc + P_BM + n * 16 + dc + 1]),
                                  reads=[pmb, prm], writes=[s_])
                            if n == 0:
                                fw.op('dve', lambda e: e.tensor_tensor(out=macc.t[:, :], in0=s_.t[:, :], in1=pub.t[:, :], op=ALU.mult),
                                      reads=[s_, pub], writes=[macc])
                            else:
                                t_ = tm_[n - 1]
                                fw.op('dve', lambda e: e.tensor_tensor(out=t_.t[:, :], in0=s_.t[:, :], in1=pub.t[:, :], op=ALU.mult),
                                      reads=[s_, pub], writes=[t_])
                                if n == 1:
                                    fw.op('dve', lambda e: e.tensor_tensor(out=macc.t[:, :], in0=macc.t[:, :], in1=t_.t[:, :], op=ALU.add),
                                          reads=[macc, t_], writes=[macc])
                                else:
                                    fw.op('dve', lambda e: e.tensor_tensor(out=merged[g].t[:, dc, :], in0=macc.t[:, :], in1=t_.t[:, :],
                                                                            op=ALU.add), reads=[macc, t_], writes=[merged[g]], nws=True)
                fw.barrier()
            ys_scope.close()
            with ExitStack() as pb_:
                wo = [sbuf(pb_, "wo%d" % i, [128, 16, 256], BF16) for i in range(2)]
                hT2 = [sbuf(pb_, "hT%d" % i, [128, 16, 512], F32) for i in range(2)]
                xr = [sbuf(pb_, "xr%d" % i, [128, 512], F32) for i in range(3)]
                hb = [sbuf(pb_, "hb%d" % i, [128, 512], BF16) for i in range(2)]
                hq = [sbuf(pb_, "hq%d" % i, [128, 512], BF16) for i in range(2)]
                mean = sbuf(pb_, "mean", [128, 512], F32)
                msq = sbuf(pb_, "msq", [128, 512], F32)
                var = sbuf(pb_, "var", [128, 512], F32)
                sdv = sbuf(pb_, "sdv", [128, 512], F32)
                rstd = sbuf(pb_, "rstd", [128, 512], F32)
                xo = [sbuf(pb_, "xo%d" % i, [128, 512], F32) for i in range(3)]
                po_ = PS[0:3]
                ps1 = PS[3]
                ps2 = PS[4]
                xsv = xsrc.t.ap().rearrange("(kc p) t -> p kc t", p=128)
                xdv = xdst.t.ap().rearrange("(kc p) t -> p kc t", p=128)
                wcnt = 0
                hTb = [[Buf(hT2[g].t, "hT%d_%d" % (g, dc)) for dc in range(16)] for g in range(2)]
                ps1g = [PS[3], PS[5]]
                ps2g = [PS[4], PS[6]]

                def emit_stats(g, dc, hb_, hq_):
                    fw.op('pe', lambda e: e.matmul(ps1g[g].t[:, :], lhsT=cba(CB_ONES), rhs=hb_.t[:, :], start=(dc == 0), stop=(dc == 15)),
                          reads=[cb, hb_], writes=[ps1g[g]])
                    fw.op('pe', lambda e: e.matmul(ps2g[g].t[:, :], lhsT=cba(CB_ONES), rhs=hq_.t[:, :], start=(dc == 0), stop=(dc == 15)),
                          reads=[cb, hq_], writes=[ps2g[g]])

                pend_stats = None
                ucnt = 0
                for dg in range(8):
                    w_ = wo[dg % 2]
                    src = w_o.t.ap()[l, :, dg * 256:(dg + 1) * 256].rearrange("(kc p) n -> p kc n", p=128)
                    fw.dma('pool', w_.t[:, :, :], src, w_o, w_, w_)
                    for dd in range(2):
                        dc = dg * 2 + dd
                        for g in glist:
                            gs_ = slice(g * 512, (g + 1) * 512)
                            hT = hT2[g]
                            pb2 = po_[ucnt % 3]
                            x_ = xr[ucnt % 3]
                            hb_, hq_ = hb[ucnt % 2], hq[ucnt % 2]
                            ucnt += 1
                            fw.dma('sp', x_.t[:, :], xsv[:, dc, gs_], xsrc, x_, x_)
                            for kc in range(16):
                                fw.op('pe', lambda e, kc=kc: e.matmul(pb2.t[:, :], lhsT=w_.t[:, kc, dd * 128:(dd + 1) * 128],
                                                                      rhs=merged[g].t[:, kc, :], start=(kc == 0), stop=(kc == 15)),
                                      reads=[w_, merged[g]], writes=[pb2], inc=(kc == 15))
                            fw.op('dve', lambda e: e.scalar_tensor_tensor(out=hT.t[:, dc, :], in0=x_.t[:, :], scalar=ALPHA, in1=pb2.t[:, :],
                                                                          op0=ALU.mult, op1=ALU.add), reads=[x_, pb2], writes=[hTb[g][dc]])
                            fw.op('act', lambda e: e.activation(out=hb_.t[:, :], in_=hT.t[:, dc, :], func=AF.Copy), reads=[hTb[g][dc]], writes=[hb_])
                            fw.op('act', lambda e: e.activation(out=hq_.t[:, :], in_=hT.t[:, dc, :], func=AF.Square), reads=[hTb[g][dc]], writes=[hq_])
                            if pend_stats is not None:
                                emit_stats(*pend_stats)
                            pend_stats = (g, dc, hb_, hq_)
                emit_stats(*pend_stats)

                for g in glist:
                    gs_ = slice(g * 512, (g + 1) * 512)
                    hT = hT2[g]
                    ps1, ps2 = ps1g[g], ps2g[g]
                    fw.op('dve', lambda e: e.tensor_scalar(out=mean.t[:, :], in0=ps1.t[:, :], scalar1=1.0 / D, scalar2=None, op0=ALU.mult),
                          reads=[ps1], writes=[mean])
                    fw.op('dve', lambda e: e.tensor_tensor(out=msq.t[:, :], in0=mean.t[:, :], in1=mean.t[:, :], op=ALU.mult),
                          reads=[mean], writes=[msq])
                    fw.op('dve', lambda e: e.scalar_tensor_tensor(out=var.t[:, :], in0=ps2.t[:, :], scalar=1.0 / D, in1=msq.t[:, :],
                                                                  op0=ALU.mult, op1=ALU.subtract), reads=[ps2, msq], writes=[var])
                    fw.op('act', lambda e: e.activation(out=sdv.t[:, :], in_=var.t[:, :], func=AF.Ln, bias=LN_EPS), reads=[var], writes=[sdv])
                    fw.op('act', lambda e: e.activation(out=rstd.t[:, :], in_=sdv.t[:, :], func=AF.Exp, scale=-0.5), reads=[sdv], writes=[rstd])
                    for dc in range(16):
                        o_ = xo[dc % 3]
                        fw.op('dve', lambda e: e.tensor_tensor(out=hT.t[:, dc, :], in0=hT.t[:, dc, :], in1=mean.t[:, :], op=ALU.subtract),
                              reads=[hTb[g][dc], mean], writes=[hTb[g][dc]])
                        fw.op('dve', lambda e: e.tensor_tensor(out=hT.t[:, dc, :], in0=hT.t[:, dc, :], in1=rstd.t[:, :], op=ALU.mult),
                              reads=[hTb[g][dc], rstd], writes=[hTb[g][dc]])
                        fw.op('act', lambda e: e.activation(out=o_.t[:, :], in_=hT.t[:, dc, :], func=AF.Identity,
                                                            scale=prm.t[:, pc + P_LG + dc:pc + P_LG + dc + 1],
                                                            bias=prm.t[:, pc + P_LB + dc:pc + P_LB + dc + 1]), reads=[hTb[g][dc], prm], writes=[o_])
                        fw.op('act', lambda e: e.activation(out=xb[g].t[:, dc, :], in_=hT.t[:, dc, :], func=AF.Identity,
                                                            scale=prm.t[:, pc + P_LG + dc:pc + P_LG + dc + 1],
                                                            bias=prm.t[:, pc + P_LB + dc:pc + P_LB + dc + 1]), reads=[hTb[g][dc], prm], writes=[xb[g]], nws=True)
                        fw.dma('sp', xdv[:, dc, gs_], o_.t[:, :], o_, xdst, o_, add=True)
                fw.barrier()

    def merge_out(l, xsrc, xdst, ysb, ys_scope, merged):
        merge_out_g(l, xsrc, xdst, [0, 1], ysb, ys_scope, merged)

    xsrc = xT_in
    for l in range(nlayers):
        xdst = y_out if l == nlayers - 1 else xres[l % 2]
        if 'p' in phases:
            projection(l)
        gen = gla(l)
        next(gen)
        swa(l)
        for _ in gen:
            pass
        with ExitStack() as mg_scope, ExitStack() as ys_scope:
            merged = {g: sbuf(mg_scope, "merged%d" % g, [128, 16, 512], BF16) for g in range(2)}
            ysb = [sbuf(ys_scope, "ysb%d" % n, [128, 8, T], BF16) for n in range(3)]
            for n in range(2):
                fw.dma('sp', ysb[n].t[:, :, :], yTs[n].t.ap().rearrange("(c p) t -> p c t", p=128), yTs[n], ysb[n], ysb[n])
            moba(l)
            merge_out(l, xsrc, xdst, ysb, ys_scope, merged)
        xsrc = xdst
    fw.barrier()
    fw.finish([y_out])
    return nc, es


def _consts(r):
    cf = np.zeros((128, NCF), np.float32)
    j = np.arange(128)[:, None]
    i = np.arange(128)[None, :]
    U = (j <= i).astype(np.float32)
    cf[:, CF_U:CF_U + 128] = U
    cf[:, CF_SL:CF_SL + 128] = (j > i).astype(np.float32)
    cf[:, CF_ONES:CF_ONES + 128] = 1.0
    cf[:, CF_ID:CF_ID + 128] = np.eye(128, dtype=np.float32)
    cf[:, CF_U4:CF_U4 + 512] = np.tile(U, (1, 4))
    pb = np.zeros((8, 16), np.float32)
    for qt in range(8):
        for n in range(16):
            pb[qt, n] = 0.0 if n < 4 * r + qt // 2 else -1e30
    cf[:, CF_PB:CF_PB + 128] = pb.reshape(1, 128)
    for rr in range(3):
        cf[:, CF_RK + rr] = 1.0 if rr < r else 0.0
    for rr in range(4):
        cf[:, CF_RK + 3 + rr] = 1.0 if rr == r - 1 else 0.0
    cb = np.zeros((128, NCB), np.float32)
    cb[:, CB_ID:CB_ID + 128] = np.eye(128)
    cb[:, CB_ONES:CB_ONES + 128] = 1.0
    cb[:, CB_CB:CB_CB + 128] = np.where(j <= i, 0.0, NEG)
    for n in range(16):
        cb[n, CB_E + n * 128:CB_E + (n + 1) * 128] = 1.0
    ii = np.arange(256)[None, :]
    band = ((ii - j >= 0) & (ii - j < 128)).astype(np.float32)
    cb[:, CB_BAND:CB_BAND + 256] = band
    cb[:, CB_BAND + 256:CB_BAND + 512] = band
    cb[:, CB_OLO:CB_OLO + 64] = 1.0
    cb[:, CB_OHI + 64:CB_OHI + 128] = 1.0
    if r > 0:
        cb[:, CB_OLOH:CB_OLOH + 64] = 1.0
        cb[:, CB_OHIH + 64:CB_OHIH + 128] = 1.0
    return cf, cb.astype(ml_dtypes.bfloat16)


def _params(gla_b, gla_norm_g, swa_sinks, b_merge, ln_g, ln_b):
    p = np.zeros((128, DEPTH * PW), np.float32)
    for l in range(DEPTH):
        o = l * PW
        p[:, o + P_GLAB:o + P_GLAB + 512] = gla_b[l][None, :]
        p[:, o + P_NG:o + P_NG + 2] = gla_norm_g[l].reshape(2, 128).T
        s = swa_sinks[l]
        p[0:64, o + P_SINK:o + P_SINK + 8] = s[None, 0:8]
        p[64:128, o + P_SINK:o + P_SINK + 8] = s[None, 8:16]
        p[:, o + P_BM:o + P_BM + 48] = b_merge[l].reshape(3, 16, 128).transpose(2, 0, 1).reshape(128, 48)
        p[:, o + P_LG:o + P_LG + 16] = ln_g[l].reshape(16, 128).T
        p[:, o + P_LB:o + P_LB + 16] = ln_b[l].reshape(16, 128).T
    return p


def make_in_maps(x, w_in, gla_w_up, gla_b, gla_norm_g, swa_sinks, b_merge, w_branch, w_o, ln_g, ln_b):
    f = lambda a: np.ascontiguousarray(np.asarray(a, dtype=np.float32))
    x, w_in, gla_w_up, w_branch, w_o = f(x), f(w_in), f(gla_w_up), f(w_branch), f(w_o)
    prm = _params(f(gla_b), f(gla_norm_g), f(swa_sinks), f(b_merge), f(ln_g), f(ln_b))
    maps = []
    for c in range(8):
        b, r = c // 4, c % 4
        cf, cb = _consts(r)
        maps.append({"xT": np.ascontiguousarray(x[b, r * T:(r + 1) * T, :].T), "w_in": w_in, "w_up": gla_w_up,
                     "w_br": w_branch, "w_o": w_o, "prm": prm, "cf": cf, "cb": cb})
    return maps


def kernel(x, w_in, gla_w_up, gla_b, gla_norm_g, swa_sinks, b_merge, w_branch, w_o, ln_g, ln_b):
    maps = make_in_maps(x, w_in, gla_w_up, gla_b, gla_norm_g, swa_sinks, b_merge, w_branch, w_o, ln_g, ln_b)
    nc, es = build()
    res = run_bass_kernel_spmd(nc, maps, core_ids=list(range(8)))
    out = np.zeros((2, 4096, D), np.float32)
    for c in range(8):
        b, r = c // 4, c % 4
        out[b, r * T:(r + 1) * T, :] = np.asarray(res.results[c]["yT"], dtype=np.float32).T
    return out
```

```python
import math
from contextlib import ExitStack
import numpy as np
import ml_dtypes
import concourse.bass as bass
import concourse.mybir as mybir
from concourse.bass_utils import run_bass_kernel_spmd

F32 = mybir.dt.float32
BF16 = mybir.dt.bfloat16
AF = mybir.ActivationFunctionType
ALU = mybir.AluOpType

DEPTH = 4
D = 2048
T = 1024
NT = 8
D_IN = 15632
O_AQ, O_AK, O_AV, O_ALR, O_AG = 0, 512, 1024, 2048, 2064
O_BQ, O_BK, O_BV, O_BG = 3088, 4112, 4240, 4368
O_CQ, O_CK, O_CV, O_CG = 5392, 6416, 7440, 8464
O_MG = 9488
ALPHA = (2 * DEPTH) ** 0.25
LN_EPS = 1e-5
RMS_EPS = 1e-6
NEG = -30000.0

CF_U, CF_SL, CF_ONES, CF_ID, CF_U4, CF_PB, CF_RK = 0, 128, 256, 384, 512, 1024, 1152
NCF = 1160
CB_ID, CB_ONES, CB_CB, CB_Z, CB_E, CB_BAND, CB_OLO, CB_OHI, CB_OLOH, CB_OHIH = 0, 128, 256, 384, 512, 2560, 3072, 3200, 3328, 3456
NCB = 3584
P_GLAB, P_NG, P_SINK, P_BM, P_LG, P_LB = 0, 512, 514, 522, 570, 586
PW = 602


class Buf:
    def __init__(self, t, name):
        self.t = t
        self.name = name
        self.w = {}
        self.r = {}


class FW:
    def __init__(self, nc, es):
        self.nc = nc
        self.es = es
        self.eng = {'pe': nc.tensor, 'act': nc.scalar, 'dve': nc.vector, 'pool': nc.gpsimd, 'sp': nc.sync}
        self.sem = {e: es.enter_context(nc.semaphore('s_' + e)) for e in self.eng}
        self.cnt = {e: 0 for e in self.eng}
        self.waited = {e: {} for e in self.eng}
        self.pend = {e: [] for e in self.eng}
        self.dsem = {}
        self.dcnt = {}
        self.ncc = 0

    def _wait(self, e, toks):
        for key, (sem, val) in toks.items():
            if e == 'pe' and key == 'pe':
                continue
            if self.waited[e].get(key, 0) < val:
                self.eng[e].wait_ge(sem, val)
                self.waited[e][key] = val

    @staticmethod
    def _merge(d, toks):
        for k, (sem, val) in toks.items():
            if k not in d or d[k][1] < val:
                d[k] = (sem, val)

    def op(self, e, fn, reads=(), writes=(), inc=True, nws=False):
        toks = {}
        for b in reads:
            self._merge(toks, b.w)
        for b in writes:
            self._merge(toks, b.w)
            self._merge(toks, b.r)
        if nws:
            toks.pop(e, None)
        self._wait(e, toks)
        inst = fn(self.eng[e])
        if not inc:
            self.pend[e].append((reads, writes))
            return inst
        self.cnt[e] += 1
        inst.then_inc(self.sem[e], 1)
        tok = {e: (self.sem[e], self.cnt[e])}
        allrw = self.pend[e] + [(reads, writes)]
        self.pend[e] = []
        for rd, wr in allrw:
            for b in rd:
                self._merge(b.r, tok)
            for b in wr:
                b.w = dict(tok)
                b.r = {}
        return inst

    def _dsem(self, b):
        if b.name not in self.dsem:
            self.dsem[b.name] = self.es.enter_context(self.nc.semaphore('d_' + b.name))
            self.dcnt[b.name] = 0
        return self.dsem[b.name]

    def dma(self, q, out_ap, in_ap, src, dst, sbside, add=False):
        toks = {}
        self._merge(toks, src.w)
        self._merge(toks, dst.r)
        if not add:
            self._merge(toks, dst.w)
        self._wait(q, toks)
        sem = self._dsem(sbside)
        inst = self.eng[q].dma_start(out=out_ap, in_=in_ap)
        self.dcnt[sbside.name] += 16
        inst.then_inc(sem, 16)
        tok = {'d:' + sbside.name: (sem, self.dcnt[sbside.name])}
        self._merge(src.r, tok)
        if add:
            self._merge(dst.w, tok)
        else:
            dst.w = dict(tok)
        dst.r = {} if not add else dst.r
        return inst

    def allgather(self, src, dst):
        toks = {}
        self._merge(toks, src.w)
        self._merge(toks, dst.r)
        self._merge(toks, dst.w)
        self._wait('pool', toks)
        self.ncc += 1
        sem = self.es.enter_context(self.nc.semaphore('cc%d' % self.ncc))
        inst = self.nc.gpsimd.collective_compute(
            "AllGather", ALU.bypass, replica_groups=[[0, 1, 2, 3], [4, 5, 6, 7]],
            ins=[src.t.ap().opt()], outs=[dst.t.ap().opt()], dma_qos="P3")
        inst.then_inc(sem)
        tok = {'cc%d' % self.ncc: (sem, 1)}
        self._merge(src.r, tok)
        dst.w = dict(tok)
        dst.r = {}

    def finish(self, bufs):
        toks = {}
        for b in bufs:
            self._merge(toks, b.w)
        self._wait('sp', toks)

    def barrier(self):
        toks = {}
        for e in self.eng:
            if self.cnt[e] > 0:
                toks[e] = (self.sem[e], self.cnt[e])
        for name, sem in self.dsem.items():
            if self.dcnt[name] > 0:
                toks['d:' + name] = (sem, self.dcnt[name])
        for e in self.eng:
            self._wait(e, {k: v for k, v in toks.items() if k != e})


def build(nlayers=DEPTH, debug=(), phases="pxgsmo", wl=DEPTH):
    nc = bass.Bass("TRN2", target_bir_lowering=False)
    es = ExitStack()
    fw = FW(nc, es)

    def dram_in(name, shape, dt):
        return nc.dram_tensor(name, shape, dt, kind="ExternalInput")

    xT_in = Buf(dram_in("xT", [D, T], F32), "xT")
    w_in = Buf(dram_in("w_in", [wl, D, D_IN], F32), "w_in")
    w_up = Buf(dram_in("w_up", [wl, 16, 512], F32), "w_up")
    w_br = Buf(dram_in("w_br", [wl, 3, 1024, D], F32), "w_br")
    w_o = Buf(dram_in("w_o", [wl, D, D], F32), "w_o")
    prm_d = Buf(dram_in("prm", [128, DEPTH * PW], F32), "prm")
    cf_d = Buf(dram_in("cf", [128, NCF], F32), "cf")
    cb_d = Buf(dram_in("cb", [128, NCB], BF16), "cb")
    y_out = Buf(nc.dram_tensor("yT", [D, T], F32, kind="ExternalOutput"), "yT")

    def scratch(name, shape, dt):
        kind = "ExternalOutput" if name in debug else None
        if kind:
            return Buf(nc.dram_tensor(name, shape, dt, kind=kind), name)
        return Buf(nc.dram_tensor(name, shape, dt), name)

    qTa = scratch("qTa", [512, T], F32)
    kTa = scratch("kTa", [512, T], F32)
    ka = scratch("ka", [T, 512], F32)
    va = scratch("va", [T, 1024], BF16)
    lrT = scratch("lrT", [16, T], BF16)
    gTa = scratch("gTa", [1024, T], F32)
    qTb = scratch("qTb", [1024, T], BF16)
    kTb = scratch("kTb", [128, T], BF16)
    vb = scratch("vb", [T, 128], BF16)
    gTb = scratch("gTb", [1024, T], F32)
    qTc = scratch("qTc", [1024, T], BF16)
    kx = [scratch("kx%d" % j, [512, T], BF16) for j in range(2)]
    vx = [scratch("vx%d" % j, [512, 1024], BF16) for j in range(2)]
    gTc = scratch("gTc", [1024, T], F32)
    kg = [scratch("kg%d" % j, [2048, T], BF16) for j in range(2)]
    vg = [scratch("vg%d" % j, [2048, 1024], BF16) for j in range(2)]
    mx = scratch("mx", [256, 128], BF16)
    mg = scratch("mg", [1024, 128], BF16)
    sx = scratch("sx", [640, 256], F32)
    sg = scratch("sg", [2560, 256], F32)
    yTs = [scratch("yT%s" % n, [1024, T], BF16) for n in "abc"]
    xres = [scratch("xres%d" % j, [D, T], F32) for j in range(2)]

    uid = [0]

    def sbuf(stack, name, shape, dt):
        uid[0] += 1
        return Buf(stack.enter_context(nc.sbuf_tensor("%s_%d" % (name, uid[0]), shape, dt)), name)

    def group(stack, name, specs):
        b = Buf(None, name)
        b.m = {}
        for k, (shape, dt) in specs.items():
            uid[0] += 1
            b.m[k] = stack.enter_context(nc.sbuf_tensor("%s_%s_%d" % (name, k, uid[0]), shape, dt))
        return b

    def psum(stack, name, shape, dt=F32):
        uid[0] += 1
        return Buf(stack.enter_context(nc.psum_tensor("%s_%d" % (name, uid[0]), shape, dt)), name)

    xb = [sbuf(es, "xb%d" % g, [128, 16, 512], BF16) for g in range(2)]
    cf = sbuf(es, "cfs", [128, NCF], F32)
    cb = sbuf(es, "cbs", [128, NCB], BF16)
    prm = sbuf(es, "prms", [128, DEPTH * PW], F32)
    esink = sbuf(es, "esink", [128, DEPTH * 8], F32)

    fw.dma('sp', cf.t[:, :], cf_d.t.ap()[:, :], cf_d, cf, cf)
    fw.dma('sp', cb.t[:, :], cb_d.t.ap()[:, :], cb_d, cb, cb)
    fw.dma('sp', prm.t[:, :], prm_d.t.ap()[:, :], prm_d, prm, prm)
    xTv = xT_in.t.ap().rearrange("(kc p) t -> p kc t", p=128)
    for g in range(2):
        for kh in range(2):
            fw.dma('pool', xb[g].t[:, kh * 8:(kh + 1) * 8, :], xTv[:, kh * 8:(kh + 1) * 8, g * 512:(g + 1) * 512],
                   xT_in, xb[g], xb[g], add=(kh > 0))
    for l in range(DEPTH):
        fw.op('act', lambda e, l=l: e.activation(out=esink.t[:, l * 8:(l + 1) * 8],
                                                 in_=prm.t[:, l * PW + P_SINK:l * PW + P_SINK + 8], func=AF.Exp),
              reads=[prm], writes=[esink])

    PS = [psum(es, "bank%d" % i, [128, 512]) for i in range(8)]

    def cfa(off, n=128):
        return cf.t[:, off:off + n]

    def cba(off, n=128):
        return cb.t[:, off:off + n]

    def projection(l):
        with ExitStack() as ph:
            wblk = [sbuf(ph, "wblk%d" % i, [128, 16, 512], BF16) for i in range(2)]
            stf = [sbuf(ph, "stf%d" % i, [128, 512], F32) for i in range(3)]
            stb = [sbuf(ph, "stb%d" % i, [128, 512], BF16) for i in range(3)]
            ps = PS[0:4]
            st = {'n': 0, 'p': 0, 'w': 0, 'e': 0}
            wv = w_in.t.ap()

            def load_w(pieces):
                wb = wblk[st['w'] % 2]
                st['w'] += 1
                first = True
                for (dc, sc, n) in pieces:
                    src = wv[l, :, sc:sc + n].rearrange("(kc p) n -> p kc n", p=128)
                    for kh in range(2):
                        fw.dma('pool', wb.t[:, kh * 8:(kh + 1) * 8, dc:dc + n], src[:, kh * 8:(kh + 1) * 8, :],
                               w_in, wb, wb, add=not first)
                        first = False
                return wb

            def evac(pb, pap, np_, ncol, dt, func, dst, dst_ap, src_view=None):
                pool = stf if dt == F32 else stb
                sb_ = pool[st['n'] % 3]
                st['n'] += 1
                use_act = (func is not None) or (st['e'] % 2 == 0)
                st['e'] += 1
                if use_act:
                    fw.op('act', lambda e: e.activation(out=sb_.t[0:np_, 0:ncol], in_=pap,
                                                        func=(func if func is not None else AF.Copy)),
                          reads=[pb], writes=[sb_])
                else:
                    fw.op('dve', lambda e: e.tensor_copy(out=sb_.t[0:np_, 0:ncol], in_=pap), reads=[pb], writes=[sb_])
                sv = sb_.t[0:np_, 0:ncol]
                fw.dma('sp', dst_ap, (src_view(sv) if src_view else sv), sb_, dst, sb_, add=True)

            def fm(pieces, chunks, dt, func, dst, dst_row):
                wb = load_w(pieces)
                for ci, (mk, m) in enumerate(chunks):
                    for g in range(2):
                        pb = ps[st['p'] % 4]
                        st['p'] += 1
                        for kc in range(16):
                            fw.op('pe', lambda e, kc=kc: e.matmul(pb.t[0:m, :], lhsT=mk(wb, kc), rhs=xb[g].t[:, kc, :],
                                                                  start=(kc == 0), stop=(kc == 15)),
                                  reads=[wb, xb[g]], writes=[pb], inc=(kc == 15))
                        r0 = dst_row(ci)
                        evac(pb, pb.t[0:m, :], m, 512, dt, func, dst, dst.t.ap()[r0:r0 + m, g * 512:(g + 1) * 512])

            def tm(pieces, ncol, dt, dst, dst_ap_fn, src_view=None):
                wb = load_w(pieces)
                for t in range(NT):
                    g, tt = t // 4, t % 4
                    pb = ps[st['p'] % 4]
                    st['p'] += 1
                    for kc in range(16):
                        fw.op('pe', lambda e, kc=kc: e.matmul(pb.t[:, 0:ncol], lhsT=xb[g].t[:, kc, tt * 128:(tt + 1) * 128],
                                                              rhs=wb.t[:, kc, 0:ncol], start=(kc == 0), stop=(kc == 15)),
                              reads=[wb, xb[g]], writes=[pb], inc=(kc == 15))
                    evac(pb, pb.t[:, 0:ncol], 128, ncol, dt, None, dst, dst_ap_fn(t), src_view)

            def plain(n):
                return [((lambda wb, kc, c=c: wb.t[:, kc, c * 128:(c + 1) * 128]), 128) for c in range(n)]

            def perm_pieces(base, j):
                pcs = []
                for ci in range(4):
                    pcs.append((ci * 128, base + (4 * j + ci) * 64, 64))
                    pcs.append((ci * 128 + 64, base + (8 + 4 * j + ci) * 64, 64))
                return pcs

            fm([(0, O_BK, 128)], plain(1), BF16, None, kTb, lambda ci: 0)
            tm([(0, O_BV, 128)], 128, BF16, vb, lambda t: vb.t.ap()[t * 128:(t + 1) * 128, :])
            for j in range(2):
                fm([(0, O_CK + j * 512, 512)], plain(4), BF16, None, kx[j], lambda ci: ci * 128)
            for j in range(2):
                tm([(0, O_CV + j * 512, 512)], 512, BF16, vx[j],
                   lambda t, j=j: vx[j].t.ap().rearrange("(h p) (t e) -> p t h e", p=128, e=128)[:, t, :, :],
                   src_view=lambda a: a.rearrange("p (h e) -> p h e", h=4))
            tm([(0, O_AK, 512)], 512, F32, ka, lambda t: ka.t.ap()[t * 128:(t + 1) * 128, :])
            for j in range(2):
                tm([(0, O_AV + j * 512, 512)], 512, BF16, va,
                   lambda t, j=j: va.t.ap()[t * 128:(t + 1) * 128, j * 512:(j + 1) * 512])
            fm([(0, O_ALR, 16)], [((lambda wb, kc: wb.t[:, kc, 0:16]), 16)], BF16, None, lrT, lambda ci: 0)
            fm([(0, O_AK, 512)], plain(4), F32, None, kTa, lambda ci: ci * 128)
            fm([(0, O_AQ, 512)], plain(4), F32, None, qTa, lambda ci: ci * 128)
            for j in range(2):
                fm([(0, O_AG + j * 512, 512)], plain(4), F32, AF.Silu, gTa, lambda ci, j=j: j * 512 + ci * 128)
            for j in range(2):
                fm(perm_pieces(O_BQ, j), plain(4), BF16, None, qTb,
                   lambda ci, j=j: (j * 4 + ci) * 128)
            for j in range(2):
                fm(perm_pieces(O_BG, j), plain(4), F32, AF.Silu, gTb,
                   lambda ci, j=j: (j * 4 + ci) * 128)
            for j in range(2):
                fm([(0, O_CQ + j * 512, 512)], plain(4), BF16, None, qTc, lambda ci, j=j: j * 512 + ci * 128)
            for j in range(2):
                fm([(0, O_CG + j * 512, 512)], plain(4), F32, AF.Silu, gTc, lambda ci, j=j: j * 512 + ci * 128)
            exchange_kv()
            fw.barrier()

    def exchange_kv():
        with ExitStack() as ph:
            hst = sbuf(ph, "hst", [128, 256], BF16)
            fw.dma('sp', hst.t[:, 0:128], kTb.t.ap()[:, T - 128:T], kTb, hst, hst)
            fw.dma('sp', hst.t[:, 128:256], vb.t.ap()[T - 128:T, :], vb, hst, hst, add=True)
            fw.dma('sp', mx.t.ap()[0:128, :], hst.t[:, 0:128], hst, mx, hst)
            fw.dma('sp', mx.t.ap()[128:256, :], hst.t[:, 128:256], hst, mx, hst, add=True)
            fw.allgather(mx, mg)

    def gla(l):
        pc = l * PW
        with ExitStack() as ph:
            wupf = sbuf(ph, "wupf", [16, 512], BF16)
            lrs = sbuf(ph, "lrs", [16, T], BF16)
            qtl = sbuf(ph, "qtl", [128, 4, T], BF16)
            ktl = sbuf(ph, "ktl", [128, 4, T], BF16)
            khat = sbuf(ph, "khat", [128, NT, 512], BF16)
            vsb = sbuf(ph, "vsb", [128, NT, 1024], BF16)
            dl = sbuf(ph, "dl", [128, NT, 4], F32)
            dtot = sbuf(ph, "dtot", [128, 4], F32)
            S = sbuf(ph, "S", [128, 4, 256], F32)
            Sb = sbuf(ph, "Sb", [128, 4, 256], BF16)
            gstg = [group(ph, "gstg%d" % i, {'q': ([128, 4, 128], F32), 'k': ([128, 4, 128], F32), 'kt': ([128, 512], F32)})
                    for i in range(2)]
            zb2 = [sbuf(ph, "zb%d" % i, [128, 512], F32) for i in range(2)]
            ex2 = [sbuf(ph, "ex%d" % i, [128, 512], F32) for i in range(2)]
            lt2 = [sbuf(ph, "lt%d" % i, [128, 512], F32) for i in range(2)]
            pz2 = [PS[0], PS[6]]
            E1s = [sbuf(ph, "E1_%d" % i, [128, 512], F32) for i in range(2)]
            E2s = [sbuf(ph, "E2_%d" % i, [128, 512], F32) for i in range(2)]
            E3s = [sbuf(ph, "E3_%d" % i, [128, 512], F32) for i in range(2)]
            pz, pcb, pr, pat = PS[0], PS[1], PS[2], PS[5]
            psu = [PS[3], PS[4]]
            po = [PS[6], PS[7], PS[1], PS[2]]

            fw.dma('pool', wupf.t[:, :], w_up.t.ap()[l, :, :], w_up, wupf, wupf)
            fw.dma('sp', lrs.t[:, :], lrT.t.ap()[:, :], lrT, lrs, lrs)
            fw.dma('sp', vsb.t[:, :, :], va.t.ap().rearrange("(t p) c -> p t c", p=128), va, vsb, vsb)
            qv = qTa.t.ap().rearrange("(h d) t -> d h t", d=128)
            kv = kTa.t.ap().rearrange("(h d) t -> d h t", d=128)
            def stepA1(t):
                ts = slice(t * 128, (t + 1) * 128)
                G_ = gstg[t % 2]
                zb, ex, lt, pz = zb2[t % 2], ex2[t % 2], lt2[t % 2], pz2[t % 2]
                fw.dma('sp', G_.m['q'][:, :, :], qv[:, :, ts], qTa, G_, G_)
                fw.dma('sp', G_.m['k'][:, :, :], kv[:, :, ts], kTa, G_, G_, add=True)
                fw.dma('sp', G_.m['kt'][:, :], ka.t.ap()[ts, :], ka, G_, G_, add=True)
                fw.op('pe', lambda e: e.matmul(pz.t[:, :], lhsT=lrs.t[:, ts], rhs=wupf.t[:, :], start=True, stop=True),
                      reads=[lrs, wupf], writes=[pz])
                fw.op('dve', lambda e: e.tensor_tensor(out=zb.t[:, :], in0=pz.t[:, :], in1=prm.t[:, pc + P_GLAB:pc + P_GLAB + 512],
                                                       op=ALU.add), reads=[pz, prm], writes=[zb])
                fw.op('act', lambda e: e.activation(out=ex.t[:, :], in_=zb.t[:, :], func=AF.Exp, scale=-1.0),
                      reads=[zb], writes=[ex])
                fw.op('act', lambda e: e.activation(out=lt.t[:, :], in_=ex.t[:, :], func=AF.Ln, bias=1.0),
                      reads=[ex], writes=[lt])

            def stepA2(t):
                ts = slice(t * 128, (t + 1) * 128)
                G_ = gstg[t % 2]
                lt = lt2[t % 2]
                E1, E2, E3 = E1s[t % 2], E2s[t % 2], E3s[t % 2]
                for h in range(4):
                    fw.op('pe', lambda e, h=h: e.matmul(pcb.t[:, h * 128:(h + 1) * 128], lhsT=lt.t[:, h * 128:(h + 1) * 128],
                                                        rhs=cfa(CF_U), start=True, stop=True),
                          reads=[lt, cf], writes=[pcb], inc=(h == 3))
                fw.op('pe', lambda e: e.matmul(pr.t[:, :], lhsT=cfa(CF_SL), rhs=lt.t[:, :], start=True, stop=True),
                      reads=[lt, cf], writes=[pr])
                fw.op('act', lambda e: e.activation(out=E1.t[:, :], in_=pcb.t[:, :], func=AF.Exp, scale=-1.0 / 16),
                      reads=[pcb], writes=[E1])
                fw.op('act', lambda e: e.activation(out=E2.t[:, :], in_=pcb.t[:, :], func=AF.Exp, scale=1.0 / 16),
                      reads=[pcb], writes=[E2])
                fw.op('act', lambda e: e.activation(out=E3.t[:, :], in_=pr.t[:, :], func=AF.Exp, scale=-1.0 / 16),
                      reads=[pr], writes=[E3])
                E1v = E1.t[:, :].rearrange("p (h i) -> p h i", h=4)
                E2v = E2.t[:, :].rearrange("p (h i) -> p h i", h=4)
                fw.op('dve', lambda e: e.scalar_tensor_tensor(out=qtl.t[:, :, ts], in0=G_.m['q'][:, :, :], scalar=128 ** -0.5,
                                                              in1=E1v, op0=ALU.mult, op1=ALU.mult),
                      reads=[G_, E1], writes=[qtl], nws=True)
                fw.op('dve', lambda e: e.tensor_tensor(out=ktl.t[:, :, ts], in0=G_.m['k'][:, :, :], in1=E2v, op=ALU.mult),
                      reads=[G_, E2], writes=[ktl], nws=True)
                fw.op('dve', lambda e: e.tensor_tensor(out=khat.t[:, t, :], in0=G_.m['kt'][:, :], in1=E3.t[:, :], op=ALU.mult),
                      reads=[G_, E3], writes=[khat], nws=True)
                fw.op('dve', lambda e: e.tensor_copy(out=dl.t[:, t, :], in_=E1v[:, :, 127]), reads=[E1], writes=[dl], nws=True)

            stepA1(0)
            for t in range(NT):
                if t + 1 < NT:
                    stepA1(t + 1)
                stepA2(t)

            def state_update(t, with_bf):
                for h in range(4):
                    pb = psu[h // 2]
                    fw.op('pe', lambda e, h=h, pb=pb: e.matmul(pb.t[:, (h % 2) * 256:(h % 2 + 1) * 256],
                                                               lhsT=khat.t[:, t, h * 128:(h + 1) * 128],
                                                               rhs=vsb.t[:, t, h * 256:(h + 1) * 256], start=True, stop=True),
                          reads=[khat, vsb], writes=[pb], inc=(h % 2 == 1))
                for h in range(4):
                    pb = psu[h // 2]
                    fw.op('dve', lambda e, h=h, pb=pb: e.scalar_tensor_tensor(
                        out=S.t[:, h, :], in0=S.t[:, h, :], scalar=dl.t[:, t, h:h + 1],
                        in1=pb.t[:, (h % 2) * 256:(h % 2 + 1) * 256], op0=ALU.mult, op1=ALU.add),
                        reads=[S, dl, pb], writes=[S], nws=(h > 0))
                if with_bf:
                    fw.op('act', lambda e: e.activation(out=Sb.t[:, :, :], in_=S.t[:, :, :], func=AF.Copy),
                          reads=[S], writes=[Sb])

            fw.op('dve', lambda e: e.memset(S.t[:, :, :], 0.0), writes=[S])
            fw.op('dve', lambda e: e.tensor_copy(out=dtot.t[:, :], in_=dl.t[:, 0, :]), reads=[dl], writes=[dtot])
            for t in range(1, NT):
                fw.op('dve', lambda e, t=t: e.tensor_tensor(out=dtot.t[:, :], in0=dtot.t[:, :], in1=dl.t[:, t, :], op=ALU.mult),
                      reads=[dl, dtot], writes=[dtot])
            for t in range(NT):
                state_update(t, False)
            zpad = sbuf(ph, "zpad", [128, 256], F32)
            fw.op('dve', lambda e: e.memset(zpad.t[:, :], 0.0), writes=[zpad])
            fw.op('dve', lambda e: e.tensor_copy(out=zpad.t[:, 0:4], in_=dtot.t[:, :]), reads=[dtot], writes=[zpad])
            fw.dma('sp', sx.t.ap()[0:512, :].rearrange("(h d) e -> d h e", d=128), S.t[:, :, :], S, sx, S)
            fw.dma('sp', sx.t.ap()[512:640, :], zpad.t[:, :], zpad, sx, zpad, add=True)
            fw.allgather(sx, sg)
            yield
            fw.allgather(kx[0], kg[0])
            fw.allgather(vx[0], vg[0])
            SL = sbuf(ph, "SLg", [128, 4, 4, 256], F32)
            Dg = sbuf(ph, "Dg", [128, 4, 256], F32)
            u = sbuf(ph, "u_", [128, 256], F32)
            sgv = sg.t.ap().rearrange("(r x) e -> x r e", x=640)
            for r in range(4):
                fw.dma('sp', SL.t[:, r, :, :], sgv[0:512, r, :].rearrange("(h d) e -> d h e", d=128), sg, SL, SL, add=(r > 0))
            fw.dma('sp', Dg.t[:, :, :], sgv[512:640, :, :], sg, Dg, Dg)
            fw.op('dve', lambda e: e.memset(S.t[:, :, :], 0.0), writes=[S])
            for r in range(3):
                for h in range(4):
                    fw.op('dve', lambda e, r=r, h=h: e.scalar_tensor_tensor(
                        out=u.t[:, :], in0=S.t[:, h, :], scalar=Dg.t[:, r, h:h + 1], in1=SL.t[:, r, h, :],
                        op0=ALU.mult, op1=ALU.add), reads=[S, Dg, SL], writes=[u])
                    fw.op('dve', lambda e, h=h: e.tensor_tensor(out=u.t[:, :], in0=u.t[:, :], in1=S.t[:, h, :], op=ALU.subtract),
                          reads=[u, S], writes=[u])
                    fw.op('dve', lambda e, r=r, h=h: e.scalar_tensor_tensor(
                        out=S.t[:, h, :], in0=u.t[:, :], scalar=cf.t[:, CF_RK + r:CF_RK + r + 1], in1=S.t[:, h, :],
                        op0=ALU.mult, op1=ALU.add), reads=[u, S, cf], writes=[S])
            fw.op('act', lambda e: e.activation(out=Sb.t[:, :, :], in_=S.t[:, :, :], func=AF.Copy), reads=[S], writes=[Sb])
            amb = sbuf(ph, "amb", [128, 512], BF16)
            sq = [sbuf(ph, "sq%d" % i, [128, 512], F32) for i in range(2)]
            sd = sbuf(ph, "sd", [128, 512], F32)
            rs = sbuf(ph, "rs", [128, 512], F32)
            t1 = sbuf(ph, "t1", [128, 8, 128], F32)
            gst = [sbuf(ph, "gst%d" % i, [128, 8, 128], F32) for i in range(2)]
            yst = [sbuf(ph, "yst%d" % i, [128, 8, 128], BF16) for i in range(2)]
            pss = pz
            gv = gTa.t.ap().rearrange("(c e) t -> e c t", e=128)
            yv = yTs[0].t.ap().rearrange("(c e) t -> e c t", e=128)
            def crit(t):
                ts = slice(t * 128, (t + 1) * 128)
                g_ = gst[t % 2]
                y_ = yst[t % 2]
                fw.dma('sp', g_.t[:, :, :], gv[:, :, ts], gTa, g_, g_)
                for h in range(4):
                    fw.op('pe', lambda e, h=h: e.matmul(pat.t[:, h * 128:(h + 1) * 128], lhsT=ktl.t[:, h, ts], rhs=qtl.t[:, h, ts],
                                                        start=True, stop=True), reads=[ktl, qtl], writes=[pat], inc=(h == 3))
                fw.op('dve', lambda e: e.tensor_tensor(out=amb.t[:, :], in0=pat.t[:, :], in1=cf.t[:, CF_U4:CF_U4 + 512], op=ALU.mult),
                      reads=[pat, cf], writes=[amb])
                pob = [po[(t % 2) * 2], po[(t % 2) * 2 + 1]]
                for h in range(4):
                    pb = pob[h // 2]
                    for ec in range(2):
                        col = ((h % 2) * 2 + ec) * 128
                        fw.op('pe', lambda e, h=h, ec=ec, pb=pb, col=col: e.matmul(
                            pb.t[:, col:col + 128], lhsT=Sb.t[:, h, ec * 128:(ec + 1) * 128], rhs=qtl.t[:, h, ts],
                            start=True, stop=False), reads=[Sb, qtl], writes=[pb], inc=False)
                        fw.op('pe', lambda e, h=h, ec=ec, pb=pb, col=col: e.matmul(
                            pb.t[:, col:col + 128], lhsT=vsb.t[:, t, h * 256 + ec * 128:h * 256 + (ec + 1) * 128],
                            rhs=amb.t[:, h * 128:(h + 1) * 128], start=False, stop=True),
                            reads=[vsb, amb], writes=[pb], inc=(h % 2 == 1 and ec == 1))
                state_update(t, True)

            def outp(t):
                ts = slice(t * 128, (t + 1) * 128)
                g_ = gst[t % 2]
                y_ = yst[t % 2]
                pob = [po[(t % 2) * 2], po[(t % 2) * 2 + 1]]
                for i in range(2):
                    fw.op('act', lambda e, i=i: e.activation(out=sq[i].t[:, :], in_=pob[i].t[:, :], func=AF.Square),
                          reads=[pob[i]], writes=[sq[i]])
                for h in range(4):
                    for ec in range(2):
                        col = ((h % 2) * 2 + ec) * 128
                        fw.op('pe', lambda e, h=h, ec=ec, col=col: e.matmul(
                            pss.t[:, h * 128:(h + 1) * 128], lhsT=cfa(CF_ONES), rhs=sq[h // 2].t[:, col:col + 128],
                            start=(ec == 0), stop=(ec == 1)), reads=[cf, sq[h // 2]], writes=[pss], inc=(h == 3 and ec == 1))
                fw.op('act', lambda e: e.activation(out=sd.t[:, :], in_=pss.t[:, :], func=AF.Ln, scale=1.0 / 256, bias=RMS_EPS),
                      reads=[pss], writes=[sd])
                fw.op('act', lambda e: e.activation(out=rs.t[:, :], in_=sd.t[:, :], func=AF.Exp, scale=-0.5), reads=[sd], writes=[rs])
                for i in range(2):
                    for ec in range(2):
                        pv = pob[i].t[:, :].rearrange("p (h c i) -> p h c i", h=2, c=2)[:, :, ec, :]
                        rv = rs.t[:, i * 256:(i + 1) * 256].rearrange("p (h i) -> p h i", h=2)
                        ov = t1.t[:, i * 4:(i + 1) * 4, :].rearrange("p (h c) i -> p h c i", h=2)[:, :, ec, :]
                        fw.op('dve', lambda e, pv=pv, rv=rv, ov=ov, ec=ec: e.scalar_tensor_tensor(
                            out=ov, in0=pv, scalar=prm.t[:, pc + P_NG + ec:pc + P_NG + ec + 1], in1=rv,
                            op0=ALU.mult, op1=ALU.mult), reads=[pob[i], rs, prm], writes=[t1], nws=(i + ec > 0))
                fw.op('dve', lambda e: e.tensor_tensor(out=y_.t[:, :, :], in0=t1.t[:, :, :], in1=g_.t[:, :, :], op=ALU.mult),
                      reads=[t1, g_], writes=[y_])
                fw.dma('sp', yv[:, :, ts], y_.t[:, :, :], y_, yTs[0], y_, add=True)

            crit(0)
            for t in range(1, NT):
                crit(t)
                outp(t - 1)
            outp(NT - 1)
            fw.barrier()

    def swa(l):
        with ExitStack() as ph:
            kts_ = sbuf(ph, "skT", [128, 9 * 128], BF16)
            vlo = sbuf(ph, "vlo", [128, 9, 128], BF16)
            vhi = sbuf(ph, "vhi", [128, 9, 128], BF16)
            hal = sbuf(ph, "hal", [128, 4, 256], BF16)
            hsum = sbuf(ph, "hsum", [128, 256], F32)
            vtmp = sbuf(ph, "vtmp", [128, 8, 128], BF16)
            sgrp = [group(ph, "sgrp%d" % i, {'q': ([128, T], BF16), 'g': ([128, T], F32)}) for i in range(2)]
            PT2 = [[sbuf(ph, "sPT%d_%d" % (j, i), [128, 512], BF16) for i in range(9)] for j in range(2)]
            den = sbuf(ph, "sden", [128, 512], F32)
            rden = sbuf(ph, "srden", [128, 512], F32)
            o1 = sbuf(ph, "so1", [128, 512], F32)
            ys = [sbuf(ph, "sys%d" % i, [128, T], BF16) for i in range(2)]
            pS = [[PS[0], PS[1]], [PS[2], PS[3]]]
            pN = [PS[4], PS[5]]
            pD = [PS[6], PS[7]]
            mgv = mg.t.ap().rearrange("(r x) c -> x r c", x=256)
            fw.dma('sp', hal.t[:, :, 0:128], mgv[0:128, :, :], mg, hal, hal)
            fw.dma('sp', hal.t[:, :, 128:256], mgv[128:256, :, :], mg, hal, hal, add=True)
            fw.op('dve', lambda e: e.tensor_scalar(out=hsum.t[:, :], in0=hal.t[:, 0, :], scalar1=cf.t[:, CF_RK + 3:CF_RK + 4],
                                                   scalar2=None, op0=ALU.mult), reads=[hal, cf], writes=[hsum])
            for r in range(1, 4):
                fw.op('dve', lambda e, r=r: e.scalar_tensor_tensor(out=hsum.t[:, :], in0=hal.t[:, r, :],
                                                                   scalar=cf.t[:, CF_RK + 3 + r:CF_RK + 4 + r],
                                                                   in1=hsum.t[:, :], op0=ALU.mult, op1=ALU.add),
                      reads=[hal, cf, hsum], writes=[hsum])
            fw.dma('sp', kts_.t[:, 128:], kTb.t.ap()[:, :], kTb, kts_, kts_)
            fw.op('dve', lambda e: e.tensor_copy(out=kts_.t[:, 0:128], in_=hsum.t[:, 0:128]), reads=[hsum, kts_], writes=[kts_])
            fw.dma('sp', vtmp.t[:, :, :], vb.t.ap().rearrange("(t p) c -> p t c", p=128), vb, vtmp, vtmp)
            fw.op('dve', lambda e: e.memset(vlo.t[:, :, :], 0.0), writes=[vlo])
            fw.op('dve', lambda e: e.memset(vhi.t[:, :, :], 0.0), writes=[vhi])
            fw.op('dve', lambda e: e.tensor_copy(out=vlo.t[:, 1:9, 0:64], in_=vtmp.t[:, :, 0:64]), reads=[vtmp, vlo], writes=[vlo])
            fw.op('dve', lambda e: e.tensor_copy(out=vhi.t[:, 1:9, 64:128], in_=vtmp.t[:, :, 64:128]), reads=[vtmp, vhi], writes=[vhi])
            fw.op('dve', lambda e: e.tensor_copy(out=vlo.t[:, 0, 0:64], in_=hsum.t[:, 128:192]), reads=[hsum, vlo], writes=[vlo])
            fw.op('dve', lambda e: e.tensor_copy(out=vhi.t[:, 0, 64:128], in_=hsum.t[:, 192:256]), reads=[hsum, vhi], writes=[vhi])
            band3 = cb.t[:, CB_BAND:CB_BAND + 512].rearrange("p (h n) -> p h n", h=2)
            scale = 64 ** -0.5
            def sw_scores(c):
                SG_, y_ = sgrp[c % 2], ys[c % 2]
                PT = PT2[c % 2]
                fw.dma('sp', SG_.m['q'][:, :], qTb.t.ap()[c * 128:(c + 1) * 128, :], qTb, SG_, SG_)
                fw.dma('sp', SG_.m['g'][:, :], gTb.t.ap()[c * 128:(c + 1) * 128, :], gTb, SG_, SG_, add=True)
                for k in range(9):
                    q0 = max(k - 1, 0)
                    q1 = min(k, 7)
                    nq = q1 - q0 + 1
                    N = nq * 128
                    for hh in range(2):
                        pb = pS[k % 2][hh]
                        fw.op('pe', lambda e, hh=hh, pb=pb: e.matmul(pb.t[:, 0:N],
                                                              lhsT=kts_.t[hh * 64:(hh + 1) * 64, k * 128:(k + 1) * 128],
                                                              rhs=SG_.m['q'][hh * 64:(hh + 1) * 64, q0 * 128:q0 * 128 + N],
                                                              start=True, stop=True), reads=[kts_, SG_], writes=[pb])
                    tv = PT[k].t[:, :].rearrange("p (h n) -> p h n", h=2)[:, :, 0:N]
                    if k == 0:
                        mv = band3[:, :, 128:256]
                    elif k == 8:
                        mv = band3[:, :, 0:128]
                    else:
                        mv = band3[:, :, 0:256]
                    for hh in range(2):
                        pb = pS[k % 2][hh]
                        fw.op('act', lambda e, hh=hh, pb=pb: e.activation(out=PT[k].t[:, hh * 256:hh * 256 + N], in_=pb.t[:, 0:N],
                                                                          func=AF.Exp, scale=scale),
                              reads=[pb], writes=[PT[k]], nws=(hh == 1))
                    fw.op('dve', lambda e, tv=tv, mv=mv: e.tensor_tensor(out=tv, in0=tv, in1=mv, op=ALU.mult),
                          reads=[PT[k], cb], writes=[PT[k]])

            def sw_pv(c):
                SG_, y_ = sgrp[c % 2], ys[c % 2]
                PT = PT2[c % 2]
                for tg in range(2):
                    pn, pd = pN[tg % 2], pD[tg % 2]
                    nmm = 0
                    for tt in range(4):
                        t = tg * 4 + tt
                        lst = []
                        for k in (t, t + 1):
                            q0 = max(k - 1, 0)
                            off = (t - q0) * 128
                            for hh in range(2):
                                lst.append((k, hh, off))
                        for idx, (k, hh, off) in enumerate(lst):
                            vsrc = vlo if hh == 0 else vhi
                            fw.op('pe', lambda e, k=k, hh=hh, off=off, vsrc=vsrc: e.matmul(
                                pn.t[:, tt * 128:(tt + 1) * 128], lhsT=vsrc.t[:, k, :], rhs=PT[k].t[:, hh * 256 + off:hh * 256 + off + 128],
                                start=(nmm == 0), stop=(nmm == 15)), reads=[vsrc, PT[k]], writes=[pn], inc=(nmm == 15))
                            nmm += 1
                    nmm = 0
                    for tt in range(4):
                        t = tg * 4 + tt
                        lst = []
                        for k in (t, t + 1):
                            q0 = max(k - 1, 0)
                            off = (t - q0) * 128
                            for hh in range(2):
                                lst.append((k, hh, off))
                        for idx, (k, hh, off) in enumerate(lst):
                            if k == 0:
                                oo = CB_OLOH if hh == 0 else CB_OHIH
                            else:
                                oo = CB_OLO if hh == 0 else CB_OHI
                            fw.op('pe', lambda e, k=k, hh=hh, off=off, oo=oo: e.matmul(
                                pd.t[:, tt * 128:(tt + 1) * 128], lhsT=cba(oo), rhs=PT[k].t[:, hh * 256 + off:hh * 256 + off + 128],
                                start=(nmm == 0), stop=(nmm == 15)), reads=[cb, PT[k]], writes=[pd], inc=(nmm == 15))
                            nmm += 1
                    gs_ = slice(tg * 512, (tg + 1) * 512)
                    fw.op('act', lambda e: e.activation(out=den.t[:, :], in_=pd.t[:, :], func=AF.Ln,
                                                        bias=esink.t[:, l * 8 + c:l * 8 + c + 1]), reads=[pd, esink], writes=[den])
                    fw.op('act', lambda e: e.activation(out=rden.t[:, :], in_=den.t[:, :], func=AF.Exp, scale=-1.0),
                          reads=[den], writes=[rden])
                    fw.op('dve', lambda e: e.tensor_tensor(out=o1.t[:, :], in0=pn.t[:, :], in1=rden.t[:, :], op=ALU.mult),
                          reads=[pn, rden], writes=[o1])
                    fw.op('dve', lambda e: e.tensor_tensor(out=y_.t[:, gs_], in0=o1.t[:, :], in1=SG_.m['g'][:, gs_], op=ALU.mult),
                          reads=[o1, SG_], writes=[y_])
                fw.dma('sp', yTs[1].t.ap()[c * 128:(c + 1) * 128, :], y_.t[:, :], y_, yTs[1], y_, add=True)

            sw_scores(0)
            for c in range(8):
                if c + 1 < 8:
                    sw_scores(c + 1)
                sw_pv(c)
            fw.barrier()

    def moba(l):
        with ExitStack() as ph:
            mgrp = [group(ph, "mgrp%d" % i, {'KT': ([128, 4, T], BF16), 'V': ([128, 4, 8, 128], BF16), 'q': ([128, T], BF16),
                                              'ko': ([128, T], BF16), 'vo': ([128, 8, 128], BF16), 'g': ([128, T], F32)})
                    for i in range(2)]
            yh = [sbuf(ph, "my%d" % i, [128, T], BF16) for i in range(2)]
            kms = sbuf(ph, "kms", [128, 16], F32)
            kmb = sbuf(ph, "kmb", [128, 16], BF16)
            gm = sbuf(ph, "gm", [128, 8, 16], F32)
            top8 = sbuf(ph, "top8", [128, 8, 8], F32)
            thr = sbuf(ph, "thr", [128, 8], F32)
            mbt = sbuf(ph, "mbt", [128, 8, 16], F32)
            mbT2 = [sbuf(ph, "mbT%d" % i, [16, T], BF16) for i in range(2)]
            PT = [sbuf(ph, "mPT%d" % i, [128, 512], BF16) for i in range(4)]
            rd = sbuf(ph, "mrd", [128, 512], F32)
            o1 = sbuf(ph, "mo1", [128, 512], F32)
            pG = PS[0]
            pTs = [PS[1], PS[2]]
            pS = [PS[3], PS[4], PS[5]]
            pN = PS[6]
            pD = PS[7]
            scale = 128 ** -0.5

            def proA(h):
                j, hh = h // 4, h % 4
                M_ = mgrp[h % 2]
                KT_t, V_t, q_t, ko_t, vo_t, g_t = (M_.m[k] for k in ('KT', 'V', 'q', 'ko', 'vo', 'g'))
                kgv = kg[j].t.ap().rearrange("(r x d) k -> d x r k", x=4, d=128)
                vgv = vg[j].t.ap().rearrange("(r x p) (t e) -> p x r t e", x=4, p=128, e=128)
                for r in range(4):
                    fw.dma('sp', KT_t[:, r, :], kgv[:, hh, r, :], kg[j], M_, M_, add=(r > 0))
                for r in range(4):
                    fw.dma('sp', V_t[:, r, :, :], vgv[:, hh, r, :, :], vg[j], M_, M_, add=True)
                fw.dma('sp', q_t[:, :], qTc.t.ap()[h * 128:(h + 1) * 128, :], qTc, M_, M_, add=True)
                fw.dma('sp', ko_t[:, :], kx[j].t.ap()[hh * 128:(hh + 1) * 128, :], kx[j], M_, M_, add=True)
                fw.dma('sp', vo_t[:, :, :], vx[j].t.ap()[hh * 128:(hh + 1) * 128, :].rearrange("p (t e) -> p t e", e=128),
                       vx[j], M_, M_, add=True)
                fw.dma('sp', g_t[:, :], gTc.t.ap()[h * 128:(h + 1) * 128, :], gTc, M_, M_, add=True)

            def proA2(h):
                M_ = mgrp[h % 2]
                KT_t = M_.m['KT']
                fw.op('dve', lambda e: e.tensor_reduce(out=kms.t[:, :], in_=KT_t[:, :, :].rearrange("p r (b k) -> p (r b) k", k=256),
                                                       axis=mybir.AxisListType.X, op=ALU.add), reads=[M_], writes=[kms])
                fw.op('dve', lambda e: e.tensor_scalar(out=kmb.t[:, :], in0=kms.t[:, :], scalar1=1.0 / 256, scalar2=None, op0=ALU.mult),
                      reads=[kms], writes=[kmb])

            def proB(h):
                M_ = mgrp[h % 2]
                q_t = M_.m['q']
                for t in range(NT):
                    fw.op('pe', lambda e, t=t: e.matmul(pG.t[:, t * 16:(t + 1) * 16], lhsT=q_t[:, t * 128:(t + 1) * 128], rhs=kmb.t[:, :],
                                                        start=True, stop=True), reads=[M_, kmb], writes=[pG], inc=(t == NT - 1))
                fw.op('dve', lambda e: e.tensor_tensor(out=gm.t[:, :, :].rearrange("p t n -> p (t n)"), in0=pG.t[:, 0:128],
                                                       in1=cf.t[:, CF_PB:CF_PB + 128], op=ALU.add), reads=[pG, cf], writes=[gm])
                for t in range(NT):
                    fw.op('dve', lambda e, t=t: e.max(out=top8.t[:, t, :], in_=gm.t[:, t, :]), reads=[gm], writes=[top8], nws=(t > 0))
                fw.op('dve', lambda e: e.tensor_scalar(out=thr.t[:, :], in0=top8.t[:, :, 2], scalar1=-1e29, scalar2=None, op0=ALU.max),
                      reads=[top8], writes=[thr])
                for t in range(NT):
                    fw.op('dve', lambda e, t=t: e.tensor_scalar(out=mbt.t[:, t, :], in0=gm.t[:, t, :], scalar1=thr.t[:, t:t + 1],
                                                                scalar2=NEG, op0=ALU.is_lt, op1=ALU.mult),
                          reads=[gm, thr], writes=[mbt], nws=(t > 0))

            def proC(h):
                mbT = mbT2[h % 2]
                for t in range(NT):
                    pt = pTs[t // 4]
                    fw.op('pe', lambda e, t=t, pt=pt: e.transpose(out=pt.t[0:16, (t % 4) * 128:(t % 4 + 1) * 128], in_=mbt.t[:, t, :],
                                                                  identity=cfa(CF_ID)),
                          reads=[mbt, cf], writes=[pt], inc=(t % 4 == 3))
                for i in range(2):
                    fw.op('act', lambda e, i=i: e.activation(out=mbT.t[:, i * 512:(i + 1) * 512], in_=pTs[i].t[0:16, :], func=AF.Copy),
                          reads=[pTs[i]], writes=[mbT])

            stc = [0]

            def main(h, hooks):
                M_, y_ = mgrp[h % 2], yh[h % 2]
                mbT = mbT2[h % 2]
                KT_t, V_t, q_t, ko_t, vo_t, g_t = (M_.m[k] for k in ('KT', 'V', 'q', 'ko', 'vo', 'g'))
                nitem = 0
                for g in range(2):
                    qs_ = slice(g * 512, (g + 1) * 512)
                    first = [True]

                    def pv_acc(ptb, vap, cs, last=False):
                        fw.op('pe', lambda e: e.matmul(pN.t[:, cs], lhsT=vap, rhs=ptb.t[:, 0:cs.stop - cs.start],
                                                       start=first[0], stop=last), reads=[ptb, M_], writes=[pN], inc=False)
                        fw.op('pe', lambda e: e.matmul(pD.t[:, cs], lhsT=cba(CB_ONES), rhs=ptb.t[:, 0:cs.stop - cs.start],
                                                       start=first[0], stop=last), reads=[ptb, cb], writes=[pD], inc=True)
                        first[0] = False

                    def stage1(kind, a):
                        pb = pS[stc[0] % 3]
                        ptb = PT[stc[0] % 4]
                        stc[0] += 1
                        if kind == 'g':
                            r, tt = a // 8, a % 8
                            n = a // 2
                            fw.op('pe', lambda e: e.matmul(pb.t[:, :], lhsT=KT_t[:, r, tt * 128:(tt + 1) * 128], rhs=q_t[:, qs_],
                                                           start=True, stop=False), reads=[M_], writes=[pb], inc=False)
                            fw.op('pe', lambda e: e.matmul(pb.t[:, :], lhsT=cb.t[0:16, CB_E + n * 128:CB_E + (n + 1) * 128],
                                                           rhs=mbT.t[:, qs_], start=False, stop=True), reads=[cb, mbT], writes=[pb])
                            fw.op('act', lambda e: e.activation(out=ptb.t[:, :], in_=pb.t[:, :], func=AF.Exp, scale=scale),
                                  reads=[pb], writes=[ptb])
                            return (ptb, V_t[:, r, tt, :], slice(0, 512))
                        elif kind == 'a':
                            ta = 2 * a
                            c0 = (ta - 4 * g) * 128
                            fw.op('pe', lambda e: e.matmul(pb.t[:, 0:256], lhsT=ko_t[:, ta * 128:(ta + 1) * 128],
                                                           rhs=q_t[:, ta * 128:(ta + 2) * 128], start=True, stop=False),
                                  reads=[M_], writes=[pb], inc=False)
                            fw.op('pe', lambda e: e.matmul(pb.t[:, 0:256], lhsT=cba(CB_ID), rhs=cb.t[:, CB_CB:CB_CB + 256],
                                                           start=False, stop=True), reads=[cb], writes=[pb])
                            fw.op('act', lambda e: e.activation(out=ptb.t[:, 0:256], in_=pb.t[:, 0:256], func=AF.Exp, scale=scale),
                                  reads=[pb], writes=[ptb])
                            return (ptb, vo_t[:, ta, :], slice(c0, c0 + 256))
                        else:
                            tb = 2 * a + 1
                            c0 = (tb - 4 * g) * 128
                            fw.op('pe', lambda e: e.matmul(pb.t[:, 0:128], lhsT=ko_t[:, tb * 128:(tb + 1) * 128],
                                                           rhs=q_t[:, tb * 128:(tb + 1) * 128], start=True, stop=False),
                                  reads=[M_], writes=[pb], inc=False)
                            fw.op('pe', lambda e: e.matmul(pb.t[:, 0:128], lhsT=cba(CB_ID), rhs=cb.t[:, CB_CB:CB_CB + 128],
                                                           start=False, stop=True), reads=[cb], writes=[pb])
                            fw.op('act', lambda e: e.activation(out=ptb.t[:, 0:128], in_=pb.t[:, 0:128], func=AF.Exp, scale=scale),
                                  reads=[pb], writes=[ptb])
                            return (ptb, vo_t[:, tb, :], slice(c0, c0 + 128))

                    items = []
                    for kt in range(28 if g == 0 else 30):
                        items.append(('g', kt))
                        if kt == 15:
                            for lb in (2 * g, 2 * g + 1):
                                items.append(('a', lb))
                                items.append(('b', lb))
                    queue = []
                    for ii, (kind, a) in enumerate(items):
                        queue.append(stage1(kind, a))
                        if len(queue) > 2:
                            pv_acc(*queue.pop(0))
                        nitem += 1
                        if nitem in hooks:
                            hooks[nitem]()
                    while queue:
                        pv = queue.pop(0)
                        pv_acc(*pv, last=(len(queue) == 0))
                    fw.op('act', lambda e: e.activation(out=o1.t[:, :], in_=pD.t[:, :], func=AF.Ln), reads=[pD], writes=[o1])
                    fw.op('act', lambda e: e.activation(out=rd.t[:, :], in_=o1.t[:, :], func=AF.Exp, scale=-1.0), reads=[o1], writes=[rd])
                    fw.op('dve', lambda e: e.tensor_tensor(out=o1.t[:, :], in0=pN.t[:, :], in1=rd.t[:, :], op=ALU.mult),
                          reads=[pN, rd], writes=[o1])
                    fw.op('dve', lambda e: e.tensor_tensor(out=y_.t[:, qs_], in0=o1.t[:, :], in1=g_t[:, qs_], op=ALU.mult),
                          reads=[o1, M_], writes=[y_])
                fw.dma('sp', yTs[2].t.ap()[h * 128:(h + 1) * 128, :], y_.t[:, :], y_, yTs[2], y_, add=True)

            proA(0)
            fw.allgather(kx[1], kg[1])
            fw.allgather(vx[1], vg[1])
            proA2(0)
            proB(0)
            proC(0)
            for h in range(8):
                hooks = {}
                if h + 1 < 8:
                    hooks = {2: (lambda h=h: proA(h + 1)), 16: (lambda h=h: proA2(h + 1)),
                             30: (lambda h=h: proB(h + 1)), 52: (lambda h=h: proC(h + 1))}
                main(h, hooks)
            fw.barrier()

    def merge_out_g(l, xsrc, xdst, glist, ysb, ys_scope, merged):
        pc = l * PW
        wv = w_in.t.ap()
        with ExitStack() as ph:
            with ExitStack() as pa:
                wset = [group(pa, "wset%d" % i, dict([('m%d' % n, ([128, 16, 128], BF16)) for n in range(3)] +
                                                      [('b%d' % n, ([128, 8, 128], BF16)) for n in range(3)])) for i in range(2)]
                sgt = [sbuf(pa, "sgt%d" % i, [128, 512], F32) for i in range(3)]
                tm_ = [sbuf(pa, "tmm%d" % i, [128, 512], F32) for i in range(2)]
                macc = sbuf(pa, "macc", [128, 512], F32)
                pm = PS[0:4]
                pu = PS[4:8]
                cnt = 0
                fw.dma('sp', ysb[2].t[:, :, :], yTs[2].t.ap().rearrange("(c p) t -> p c t", p=128), yTs[2], ysb[2], ysb[2])
                for dc in range(16):
                    W_ = wset[dc % 2]
                    firstw = True
                    for n in range(3):
                        c0 = O_MG + n * D + dc * 128
                        src = wv[l, :, c0:c0 + 128].rearrange("(kc p) n -> p kc n", p=128)
                        fw.dma('pool', W_.m['m%d' % n][:, :, :], src, w_in, W_, W_, add=not firstw)
                        firstw = False
                        if n == 1:
                            for half in range(2):
                                srcb = w_br.t.ap()[l, 1, half * 512:(half + 1) * 512, dc * 128:(dc + 1) * 128].rearrange(
                                    "(c p) n -> p c n", p=64)
                                fw.dma('pool', W_.m['b1'][half * 64:(half + 1) * 64, :, :], srcb, w_br, W_, W_, add=True)
                        else:
                            srcb = w_br.t.ap()[l, n, :, dc * 128:(dc + 1) * 128].rearrange("(c p) n -> p c n", p=128)
                            fw.dma('pool', W_.m['b%d' % n][:, :, :], srcb, w_br, W_, W_, add=True)
                    for g in glist:
                        gs_ = slice(g * 512, (g + 1) * 512)
                        for n in range(3):
                            pmb, pub = pm[cnt % 4], pu[cnt % 4]
                            cnt += 1
                            for kc in range(16):
                                fw.op('pe', lambda e, kc=kc: e.matmul(pmb.t[:, :], lhsT=W_.m['m%d' % n][:, kc, :],
                                                                      rhs=xb[g].t[:, kc, :], start=(kc == 0), stop=(kc == 15)),
                                      reads=[W_, xb[g]], writes=[pmb], inc=(kc == 15))
                            for wc in range(8):
                                fw.op('pe', lambda e, wc=wc: e.matmul(pub.t[:, :], lhsT=W_.m['b%d' % n][:, wc, :],
                                                                      rhs=ysb[n].t[:, wc, gs_], start=(wc == 0), stop=(wc == 7)),
                                      reads=[W_, ysb[n]], writes=[pub], inc=(wc == 7))
                            s_ = sgt[n]
                            fw.op('act', lambda e: e.activation(out=s_.t[:, :], in_=pmb.t[:, :], func=AF.Sigmoid,
                                                                bias=prm.t[:, pc + P_BM + n * 16 + dc:pc + P_BM + n * 16 + dc + 1]),
                                  reads=[pmb, prm], writes=[s_])
                            if n == 0:
                                fw.op('dve', lambda e: e.tensor_tensor(out=macc.t[:, :], in0=s_.t[:, :], in1=pub.t[:, :], op=ALU.mult),
                                      reads=[s_, pub], writes=[macc])
                            else:
                                t_ = tm_[n - 1]
                                fw.op('dve', lambda e: e.tensor_tensor(out=t_.t[:, :], in0=s_.t[:, :], in1=pub.t[:, :], op=ALU.mult),
                                      reads=[s_, pub], writes=[t_])
                                if n == 1:
                                    fw.op('dve', lambda e: e.tensor_tensor(out=macc.t[:, :], in0=macc.t[:, :], in1=t_.t[:, :], op=ALU.add),
                                          reads=[macc, t_], writes=[macc])
                                else:
                                    fw.op('dve', lambda e: e.tensor_tensor(out=merged[g].t[:, dc, :], in0=macc.t[:, :], in1=t_.t[:, :],
                                                                            op=ALU.add), reads=[macc, t_], writes=[merged[g]], nws=True)
                fw.barrier()
            ys_scope.close()
            with ExitStack() as pb_:
                wo = [sbuf(pb_, "wo%d" % i, [128, 16, 256], BF16) for i in range(2)]
                hT2 = [sbuf(pb_, "hT%d" % i, [128, 16, 512], F32) for i in range(2)]
                xr = [sbuf(pb_, "xr%d" % i, [128, 512], F32) for i in range(3)]
                hb = [sbuf(pb_, "hb%d" % i, [128, 512], BF16) for i in range(2)]
                hq = [sbuf(pb_, "hq%d" % i, [128, 512], BF16) for i in range(2)]
                mean = sbuf(pb_, "mean", [128, 512], F32)
                msq = sbuf(pb_, "msq", [128, 512], F32)
                var = sbuf(pb_, "var", [128, 512], F32)
                sdv = sbuf(pb_, "sdv", [128, 512], F32)
                rstd = sbuf(pb_, "rstd", [128, 512], F32)
                xo = [sbuf(pb_, "xo%d" % i, [128, 512], F32) for i in range(3)]
                po_ = PS[0:3]
                ps1 = PS[3]
                ps2 = PS[4]
                xsv = xsrc.t.ap().rearrange("(kc p) t -> p kc t", p=128)
                xdv = xdst.t.ap().rearrange("(kc p) t -> p kc t", p=128)
                wcnt = 0
                hTb = [[Buf(hT2[g].t, "hT%d_%d" % (g, dc)) for dc in range(16)] for g in range(2)]
                ps1g = [PS[3], PS[5]]
                ps2g = [PS[4], PS[6]]

                def emit_stats(g, dc, hb_, hq_):
                    fw.op('pe', lambda e: e.matmul(ps1g[g].t[:, :], lhsT=cba(CB_ONES), rhs=hb_.t[:, :], start=(dc == 0), stop=(dc == 15)),
                          reads=[cb, hb_], writes=[ps1g[g]])
                    fw.op('pe', lambda e: e.matmul(ps2g[g].t[:, :], lhsT=cba(CB_ONES), rhs=hq_.t[:, :], start=(dc == 0), stop=(dc == 15)),
                          reads=[cb, hq_], writes=[ps2g[g]])

                pend_stats = None
                ucnt = 0
                for dg in range(8):
                    w_ = wo[dg % 2]
                    src = w_o.t.ap()[l, :, dg * 256:(dg + 1) * 256].rearrange("(kc p) n -> p kc n", p=128)
                    fw.dma('pool', w_.t[:, :, :], src, w_o, w_, w_)
                    for dd in range(2):
                        dc = dg * 2 + dd
                        for g in glist:
                            gs_ = slice(g * 512, (g + 1) * 512)
                            hT = hT2[g]
                            pb2 = po_[ucnt % 3]
                            x_ = xr[ucnt % 3]
                            hb_, hq_ = hb[ucnt % 2], hq[ucnt % 2]
                            ucnt += 1
                            fw.dma('sp', x_.t[:, :], xsv[:, dc, gs_], xsrc, x_, x_)
                            for kc in range(16):
                                fw.op('pe', lambda e, kc=kc: e.matmul(pb2.t[:, :], lhsT=w_.t[:, kc, dd * 128:(dd + 1) * 128],
                                                                      rhs=merged[g].t[:, kc, :], start=(kc == 0), stop=(kc == 15)),
                                      reads=[w_, merged[g]], writes=[pb2], inc=(kc == 15))
                            fw.op('dve', lambda e: e.scalar_tensor_tensor(out=hT.t[:, dc, :], in0=x_.t[:, :], scalar=ALPHA, in1=pb2.t[:, :],
                                                                          op0=ALU.mult, op1=ALU.add), reads=[x_, pb2], writes=[hTb[g][dc]])
                            fw.op('act', lambda e: e.activation(out=hb_.t[:, :], in_=hT.t[:, dc, :], func=AF.Copy), reads=[hTb[g][dc]], writes=[hb_])
                            fw.op('act', lambda e: e.activation(out=hq_.t[:, :], in_=hT.t[:, dc, :], func=AF.Square), reads=[hTb[g][dc]], writes=[hq_])
                            if pend_stats is not None:
                                emit_stats(*pend_stats)
                            pend_stats = (g, dc, hb_, hq_)
                emit_stats(*pend_stats)

                for g in glist:
                    gs_ = slice(g * 512, (g + 1) * 512)
                    hT = hT2[g]
                    ps1, ps2 = ps1g[g], ps2g[g]
                    fw.op('dve', lambda e: e.tensor_scalar(out=mean.t[:, :], in0=ps1.t[:, :], scalar1=1.0 / D, scalar2=None, op0=ALU.mult),
                          reads=[ps1], writes=[mean])
                    fw.op('dve', lambda e: e.tensor_tensor(out=msq.t[:, :], in0=mean.t[:, :], in1=mean.t[:, :], op=ALU.mult),
                          reads=[mean], writes=[msq])
                    fw.op('dve', lambda e: e.scalar_tensor_tensor(out=var.t[:, :], in0=ps2.t[:, :], scalar=1.0 / D, in1=msq.t[:, :],
                                                                  op0=ALU.mult, op1=ALU.subtract), reads=[ps2, msq], writes=[var])
                    fw.op('act', lambda e: e.activation(out=sdv.t[:, :], in_=var.t[:, :], func=AF.Ln, bias=LN_EPS), reads=[var], writes=[sdv])
                    fw.op('act', lambda e: e.activation(out=rstd.t[:, :], in_=sdv.t[:, :], func=AF.Exp, scale=-0.5), reads=[sdv], writes=[rstd])
                    for dc in range(16):
                        o_ = xo[dc % 3]
                        fw.op('dve', lambda e: e.tensor_tensor(out=hT.t[:, dc, :], in0=hT.t[:, dc, :], in1=mean.t[:, :], op=ALU.subtract),
                              reads=[hTb[g][dc], mean], writes=[hTb[g][dc]])
                        fw.op('dve', lambda e: e.tensor_tensor(out=hT.t[:, dc, :], in0=hT.t[:, dc, :], in1=rstd.t[:, :], op=ALU.mult),
                              reads=[hTb[g][dc], rstd], writes=[hTb[g][dc]])
                        fw.op('act', lambda e: e.activation(out=o_.t[:, :], in_=hT.t[:, dc, :], func=AF.Identity,
                                                            scale=prm.t[:, pc + P_LG + dc:pc + P_LG + dc + 1],
                                                            bias=prm.t[:, pc + P_LB + dc:pc + P_LB + dc + 1]), reads=[hTb[g][dc], prm], writes=[o_])
                        fw.op('act', lambda e: e.activation(out=xb[g].t[:, dc, :], in_=hT.t[:, dc, :], func=AF.Identity,
                                                            scale=prm.t[:, pc + P_LG + dc:pc + P_LG + dc + 1],
                                                            bias=prm.t[:, pc + P_LB + dc:pc + P_LB + dc + 1]), reads=[hTb[g][dc], prm], writes=[xb[g]], nws=True)
                        fw.dma('sp', xdv[:, dc, gs_], o_.t[:, :], o_, xdst, o_, add=True)
                fw.barrier()

    def merge_out(l, xsrc, xdst, ysb, ys_scope, merged):
        merge_out_g(l, xsrc, xdst, [0, 1], ysb, ys_scope, merged)

    xsrc = xT_in
    for l in range(nlayers):
        xdst = y_out if l == nlayers - 1 else xres[l % 2]
        if 'p' in phases:
            projection(l)
        gen = gla(l)
        next(gen)
        swa(l)
        for _ in gen:
            pass
        with ExitStack() as mg_scope, ExitStack() as ys_scope:
            merged = {g: sbuf(mg_scope, "merged%d" % g, [128, 16, 512], BF16) for g in range(2)}
            ysb = [sbuf(ys_scope, "ysb%d" % n, [128, 8, T], BF16) for n in range(3)]
            for n in range(2):
                fw.dma('sp', ysb[n].t[:, :, :], yTs[n].t.ap().rearrange("(c p) t -> p c t", p=128), yTs[n], ysb[n], ysb[n])
            moba(l)
            merge_out(l, xsrc, xdst, ysb, ys_scope, merged)
        xsrc = xdst
    fw.barrier()
    fw.finish([y_out])
    return nc, es


def _consts(r):
    cf = np.zeros((128, NCF), np.float32)
    j = np.arange(128)[:, None]
    i = np.arange(128)[None, :]
    U = (j <= i).astype(np.float32)
    cf[:, CF_U:CF_U + 128] = U
    cf[:, CF_SL:CF_SL + 128] = (j > i).astype(np.float32)
    cf[:, CF_ONES:CF_ONES + 128] = 1.0
    cf[:, CF_ID:CF_ID + 128] = np.eye(128, dtype=np.float32)
    cf[:, CF_U4:CF_U4 + 512] = np.tile(U, (1, 4))
    pb = np.zeros((8, 16), np.float32)
    for qt in range(8):
        for n in range(16):
            pb[qt, n] = 0.0 if n < 4 * r + qt // 2 else -1e30
    cf[:, CF_PB:CF_PB + 128] = pb.reshape(1, 128)
    for rr in range(3):
        cf[:, CF_RK + rr] = 1.0 if rr < r else 0.0
    for rr in range(4):
        cf[:, CF_RK + 3 + rr] = 1.0 if rr == r - 1 else 0.0
    cb = np.zeros((128, NCB), np.float32)
    cb[:, CB_ID:CB_ID + 128] = np.eye(128)
    cb[:, CB_ONES:CB_ONES + 128] = 1.0
    cb[:, CB_CB:CB_CB + 128] = np.where(j <= i, 0.0, NEG)
    for n in range(16):
        cb[n, CB_E + n * 128:CB_E + (n + 1) * 128] = 1.0
    ii = np.arange(256)[None, :]
    band = ((ii - j >= 0) & (ii - j < 128)).astype(np.float32)
    cb[:, CB_BAND:CB_BAND + 256] = band
    cb[:, CB_BAND + 256:CB_BAND + 512] = band
    cb[:, CB_OLO:CB_OLO + 64] = 1.0
    cb[:, CB_OHI + 64:CB_OHI + 128] = 1.0
    if r > 0:
        cb[:, CB_OLOH:CB_OLOH + 64] = 1.0
        cb[:, CB_OHIH + 64:CB_OHIH + 128] = 1.0
    return cf, cb.astype(ml_dtypes.bfloat16)


def _params(gla_b, gla_norm_g, swa_sinks, b_merge, ln_g, ln_b):
    p = np.zeros((128, DEPTH * PW), np.float32)
    for l in range(DEPTH):
        o = l * PW
        p[:, o + P_GLAB:o + P_GLAB + 512] = gla_b[l][None, :]
        p[:, o + P_NG:o + P_NG + 2] = gla_norm_g[l].reshape(2, 128).T
        s = swa_sinks[l]
        p[0:64, o + P_SINK:o + P_SINK + 8] = s[None, 0:8]
        p[64:128, o + P_SINK:o + P_SINK + 8] = s[None, 8:16]
        p[:, o + P_BM:o + P_BM + 48] = b_merge[l].reshape(3, 16, 128).transpose(2, 0, 1).reshape(128, 48)
        p[:, o + P_LG:o + P_LG + 16] = ln_g[l].reshape(16, 128).T
        p[:, o + P_LB:o + P_LB + 16] = ln_b[l].reshape(16, 128).T
    return p


def make_in_maps(x, w_in, gla_w_up, gla_b, gla_norm_g, swa_sinks, b_merge, w_branch, w_o, ln_g, ln_b):
    f = lambda a: np.ascontiguousarray(np.asarray(a, dtype=np.float32))
    x, w_in, gla_w_up, w_branch, w_o = f(x), f(w_in), f(gla_w_up), f(w_branch), f(w_o)
    prm = _params(f(gla_b), f(gla_norm_g), f(swa_sinks), f(b_merge), f(ln_g), f(ln_b))
    maps = []
    for c in range(8):
        b, r = c // 4, c % 4
        cf, cb = _consts(r)
        maps.append({"xT": np.ascontiguousarray(x[b, r * T:(r + 1) * T, :].T), "w_in": w_in, "w_up": gla_w_up,
                     "w_br": w_branch, "w_o": w_o, "prm": prm, "cf": cf, "cb": cb})
    return maps


def kernel(x, w_in, gla_w_up, gla_b, gla_norm_g, swa_sinks, b_merge, w_branch, w_o, ln_g, ln_b):
    maps = make_in_maps(x, w_in, gla_w_up, gla_b, gla_norm_g, swa_sinks, b_merge, w_branch, w_o, ln_g, ln_b)
    nc, es = build()
    res = run_bass_kernel_spmd(nc, maps, core_ids=list(range(8)))
    out = np.zeros((2, 4096, D), np.float32)
    for c in range(8):
        b, r = c // 4, c % 4
        out[b, r * T:(r + 1) * T, :] = np.asarray(res.results[c]["yT"], dtype=np.float32).T
    return out
```
